# Optimizing a Trainium2 kernel written in Bass

```python
import math
import jax, jax.numpy as jnp
from jax import lax
import numpy as np

D_MODEL = 1024
BATCH = 8
SEQ = 2048
DEPTH = 1
DEC_BATCH = 32
DEC_SEQ = 1
PAST_LEN = 8192
PAGE_SIZE = 128

N_META = 16
ATTN_WIDTH = D_MODEL // 2
CONV_DIM = D_MODEL - ATTN_WIDTH
N_DIFF_HEADS = 4
HEAD_DIM = ATTN_WIDTH // N_DIFF_HEADS // 2
V_HEAD_DIM = 2 * HEAD_DIM
CONV_WIDTH = 31
FFN_HIDDEN = -(-8 * D_MODEL // (3 * 256)) * 256
IN_COLS = 3 * ATTN_WIDTH + 2 * CONV_DIM
Q_BLOCK = 128
EPS = 1e-6
NEG_INF = -1e30

kernel_name = "hymba_diffattn_conformer_decoder_step"


def rms_norm(x, w):
    xf = x.astype(jnp.float32)
    y = xf * lax.rsqrt(jnp.mean(xf * xf, axis=-1, keepdims=True) + EPS)
    return (y * w.astype(jnp.float32)).astype(x.dtype)


def layer_norm(x, w, b):
    xf = x.astype(jnp.float32)
    mu = jnp.mean(xf, axis=-1, keepdims=True)
    xc = xf - mu
    y = xc * lax.rsqrt(jnp.mean(xc * xc, axis=-1, keepdims=True) + EPS)
    return (y * w.astype(jnp.float32) + b.astype(jnp.float32)).astype(x.dtype)


def alibi_slopes():
    h = jnp.arange(1, N_DIFF_HEADS + 1, dtype=jnp.float32)
    return jnp.exp2(-8.0 * h / N_DIFF_HEADS)


def lambda_init_of(li):
    return 0.8 - 0.6 * math.exp(-0.3 * li)


def diff_lambda(lw, li):
    f = jnp.float32
    return (jnp.exp(jnp.sum(lw['lq1'].astype(f) * lw['lk1'].astype(f)))
            - jnp.exp(jnp.sum(lw['lq2'].astype(f) * lw['lk2'].astype(f)))
            + lambda_init_of(li))


def split_proj(p):
    B, L, _ = p.shape
    q = p[..., :ATTN_WIDTH].reshape(B, L, N_DIFF_HEADS, 2, HEAD_DIM)
    k = p[..., ATTN_WIDTH:2 * ATTN_WIDTH].reshape(B, L, N_DIFF_HEADS, 2, HEAD_DIM)
    v = p[..., 2 * ATTN_WIDTH:3 * ATTN_WIDTH].reshape(B, L, N_DIFF_HEADS, V_HEAD_DIM)
    a = p[..., 3 * ATTN_WIDTH:3 * ATTN_WIDTH + CONV_DIM]
    g = p[..., 3 * ATTN_WIDTH + CONV_DIM:]
    u = a * jax.nn.sigmoid(g)
    return q, k, v, u


def diff_attend(q, k, v, q_pos, k_pos, lam, slopes):
    s = jnp.einsum('bqhjd,bkhjd->bhjqk', q, k,
                   preferred_element_type=jnp.float32) * (HEAD_DIM ** -0.5)
    dist = (q_pos[:, None] - k_pos[None, :]).astype(jnp.float32)
    s = s - slopes[None, :, None, None, None] * dist
    s = jnp.where((k_pos[None, :] <= q_pos[:, None])[None, None, None], s, NEG_INF)
    p = jax.nn.softmax(s, axis=-1)
    a = p[:, :, 0] - lam * p[:, :, 1]
    return jnp.einsum('bhqk,bkhe->bqhe', a.astype(v.dtype), v)


def diff_head_out(o, lw, li):
    o = rms_norm(o, lw['subln']) * (1.0 - lambda_init_of(li))
    return o.reshape(o.shape[:2] + (ATTN_WIDTH,))


def causal_depthwise(u_padded, w, b):
    y = lax.conv_general_dilated(u_padded, w[:, None, :].astype(u_padded.dtype),
                                 window_strides=(1,), padding='VALID',
                                 dimension_numbers=('NWC', 'WIO', 'NWC'),
                                 feature_group_count=CONV_DIM)
    return y + b


def finish_layer(x_res, attn_o, conv_y, lw, li):
    attn_out = diff_head_out(attn_o, lw, li)
    conv_out = jax.nn.silu(layer_norm(conv_y, lw['cln_w'], lw['cln_b']))
    mix = jnp.concatenate([attn_out, conv_out], axis=-1) @ lw['w_out']
    x = x_res + rms_norm(mix, lw['ln_mix_post'])
    h = rms_norm(x, lw['ln_ffn_pre'])
    f = (jax.nn.silu(h @ lw['w_gate']) * (h @ lw['w_up'])) @ lw['w_down']
    return x + rms_norm(f, lw['ln_ffn_post'])


def prompt_layer(x, lw, li, last):
    B, T, _ = x.shape
    n_real = T - N_META
    nb = n_real // Q_BLOCK
    proj = rms_norm(x, lw['ln_mix_pre']) @ lw['w_in']
    q, k, v, u = split_proj(proj)
    lam = diff_lambda(lw, li)
    slopes = alibi_slopes()
    k_pos = jnp.arange(T)
    q_blocks = q[:, N_META:].reshape(B, nb, Q_BLOCK, N_DIFF_HEADS, 2, HEAD_DIM).transpose(1, 0, 2, 3, 4, 5)
    qpos_blocks = (N_META + jnp.arange(n_real)).reshape(nb, Q_BLOCK)
    o_blocks = lax.map(lambda a: diff_attend(a[0], k, v, a[1], k_pos, lam, slopes),
                       (q_blocks, qpos_blocks))
    o = o_blocks.transpose(1, 0, 2, 3, 4).reshape(B, n_real, N_DIFF_HEADS, V_HEAD_DIM)
    r0 = N_META
    if not last:
        o_meta = diff_attend(q[:, :N_META], k[:, :N_META], v[:, :N_META],
                             k_pos[:N_META], k_pos[:N_META], lam, slopes)
        o = jnp.concatenate([o_meta, o], axis=1)
        r0 = 0
    u_pad = jnp.pad(u, ((0, 0), (CONV_WIDTH - 1, 0), (0, 0)))
    y = causal_depthwise(u_pad, lw['conv_w'], lw['conv_b'])
    x_out = finish_layer(x[:, r0:], o, y[:, r0:], lw, li)
    new_k = k.reshape(B, T, 2 * N_DIFF_HEADS, HEAD_DIM)
    new_conv = u[:, T - (CONV_WIDTH - 1):]
    return x_out, new_k, v, new_conv


def sample_layer(x, ck, cv, conv_state, page_table, lw, li):
    Bd, S, _ = x.shape
    past = page_table.shape[1] * PAGE_SIZE
    proj = rms_norm(x, lw['ln_mix_pre']) @ lw['w_in']
    q, k, v, u = split_proj(proj)
    lam = diff_lambda(lw, li)
    k_past = ck[page_table].reshape(Bd, past, N_DIFF_HEADS, 2, HEAD_DIM).astype(k.dtype)
    v_past = cv[page_table].reshape(Bd, past, N_DIFF_HEADS, V_HEAD_DIM).astype(v.dtype)
    k_all = jnp.concatenate([k_past, k], axis=1)
    v_all = jnp.concatenate([v_past, v], axis=1)
    k_pos = jnp.arange(past + S)
    q_pos = past + jnp.arange(S)
    o = diff_attend(q, k_all, v_all, q_pos, k_pos, lam, alibi_slopes())
    u_full = jnp.concatenate([conv_state.astype(u.dtype), u], axis=1)
    y = causal_depthwise(u_full, lw['conv_w'], lw['conv_b'])
    x_out = finish_layer(x, o, y, lw, li)
    new_k = k.reshape(Bd, S, 2 * N_DIFF_HEADS, HEAD_DIM)
    new_conv = u_full[:, S:]
    return x_out, new_k, v, new_conv


def setup_inputs(seed: int = 0) -> dict:
    key = jax.random.key(seed)
    ks = jax.random.split(key, 32)
    f = jnp.float32
    n_pages = PAST_LEN // PAGE_SIZE
    n_used = DEC_BATCH * n_pages
    n_pool = n_used + max(1, n_used // 4)
    page_table = jax.random.permutation(ks[0], n_pool)[:n_used].reshape(DEC_BATCH, n_pages).astype(jnp.int32)
    nrm = lambda i, shape, s: (jax.random.normal(ks[i], shape, f) * s).astype(f)
    gain = lambda i, shape: (1.0 + 0.05 * jax.random.normal(ks[i], shape, f)).astype(f)
    return {
        'x_prompt': nrm(1, (BATCH, SEQ, D_MODEL), 1.0),
        'x_sample': nrm(2, (DEC_BATCH, DEC_SEQ, D_MODEL), 1.0),
        'cache_k': nrm(3, (DEPTH, n_pool, PAGE_SIZE, 2 * N_DIFF_HEADS, HEAD_DIM), 1.0),
        'cache_v': nrm(4, (DEPTH, n_pool, PAGE_SIZE, N_DIFF_HEADS, V_HEAD_DIM), 1.0),
        'state_conv': nrm(5, (DEPTH, DEC_BATCH, CONV_WIDTH - 1, CONV_DIM), 0.5),
        'page_table': page_table,
        'meta_tokens': nrm(6, (N_META, D_MODEL), 1.0),
        'ln_mix_pre': gain(7, (DEPTH, D_MODEL)),
        'ln_mix_post': gain(8, (DEPTH, D_MODEL)),
        'w_in': nrm(9, (DEPTH, D_MODEL, IN_COLS), D_MODEL ** -0.5),
        'lambda_q1': nrm(10, (DEPTH, HEAD_DIM), 0.1),
        'lambda_k1': nrm(11, (DEPTH, HEAD_DIM), 0.1),
        'lambda_q2': nrm(12, (DEPTH, HEAD_DIM), 0.1),
        'lambda_k2': nrm(13, (DEPTH, HEAD_DIM), 0.1),
        'subln_w': gain(14, (DEPTH, V_HEAD_DIM)),
        'conv_w': nrm(15, (DEPTH, CONV_WIDTH, CONV_DIM), CONV_WIDTH ** -0.5),
        'conv_b': nrm(16, (DEPTH, CONV_DIM), 0.02),
        'conv_ln_w': gain(17, (DEPTH, CONV_DIM)),
        'conv_ln_b': nrm(18, (DEPTH, CONV_DIM), 0.02),
        'w_out': nrm(19, (DEPTH, D_MODEL, D_MODEL), D_MODEL ** -0.5),
        'ln_ffn_pre': gain(20, (DEPTH, D_MODEL)),
        'ln_ffn_post': gain(21, (DEPTH, D_MODEL)),
        'w_gate': nrm(22, (DEPTH, D_MODEL, FFN_HIDDEN), D_MODEL ** -0.5),
        'w_up': nrm(23, (DEPTH, D_MODEL, FFN_HIDDEN), D_MODEL ** -0.5),
        'w_down': nrm(24, (DEPTH, FFN_HIDDEN, D_MODEL), FFN_HIDDEN ** -0.5),
    }


def reference(x_prompt, x_sample, cache_k, cache_v, state_conv, page_table, meta_tokens,
              ln_mix_pre, ln_mix_post, w_in, lambda_q1, lambda_k1, lambda_q2, lambda_k2,
              subln_w, conv_w, conv_b, conv_ln_w, conv_ln_b, w_out, ln_ffn_pre, ln_ffn_post,
              w_gate, w_up, w_down):
    B = x_prompt.shape[0]
    meta = jnp.broadcast_to(meta_tokens[None].astype(x_prompt.dtype), (B, N_META, D_MODEL))
    xp = jnp.concatenate([meta, x_prompt], axis=1)
    xs = x_sample
    kp_l, vp_l, cp_l, ks_l, vs_l, cs_l = [], [], [], [], [], []
    for li in range(DEPTH):
        lw = dict(ln_mix_pre=ln_mix_pre[li], ln_mix_post=ln_mix_post[li], w_in=w_in[li],
                  lq1=lambda_q1[li], lk1=lambda_k1[li], lq2=lambda_q2[li], lk2=lambda_k2[li],
                  subln=subln_w[li], conv_w=conv_w[li], conv_b=conv_b[li],
                  cln_w=conv_ln_w[li], cln_b=conv_ln_b[li], w_out=w_out[li],
                  ln_ffn_pre=ln_ffn_pre[li], ln_ffn_post=ln_ffn_post[li],
                  w_gate=w_gate[li], w_up=w_up[li], w_down=w_down[li])
        last = li == DEPTH - 1
        xp, kp, vp, cp = prompt_layer(xp, lw, li, last)
        xs, ks_, vs_, cs_ = sample_layer(xs, cache_k[li], cache_v[li], state_conv[li], page_table, lw, li)
        kp_l.append(kp); vp_l.append(vp); cp_l.append(cp)
        ks_l.append(ks_); vs_l.append(vs_); cs_l.append(cs_)
    return (xp, xs, jnp.stack(kp_l), jnp.stack(vp_l), jnp.stack(cp_l),
            jnp.stack(ks_l), jnp.stack(vs_l), jnp.stack(cs_l))
```

```python
GEN = 4000
DBG_SKIP = set()
DBG_SKIP_METH = set()
_SEM_STACKS = []
NDMA = 12


class _Rec:
    def __getattr__(self, name):
        def f(*a, **k):
            return (name, a, k)
        return f
R = _Rec()

class Op:
    __slots__ = ("eng", "fn", "deps", "dma", "idx", "need_inc", "cnt", "dsem", "dval", "prev_dma")

    def __init__(self, eng, fn, dma):
        self.eng = eng
        self.fn = fn
        self.dma = dma
        self.deps = set()
        self.need_inc = False
        self.cnt = None
        self.dsem = None
        self.dval = None
        self.prev_dma = None


class Sched:
    ENGS = ("pe", "act", "dve", "pool", "sp")

    def __init__(self, nc, same_engine_sync=True):
        self.nc = nc
        self.ops = []
        self.last_w = {}
        self.readers = {}
        self.same = same_engine_sync

    def add(self, eng, fn, reads=(), writes=(), dma=False):
        if any(k in DBG_SKIP for k in writes) or (DBG_SKIP_METH and fn[0] in DBG_SKIP_METH):
            return None
        op = Op(eng, fn, dma)
        op.idx = len(self.ops)
        for k in reads:
            w = self.last_w.get(k)
            if w is not None:
                op.deps.add(w)
            if k[:2] in ("pj", "pt", "ps", "pO", "pf"):
                for r in self.readers.get(k, ()):
                    op.deps.add(r)
        for k in writes:
            w = self.last_w.get(k)
            if w is not None:
                op.deps.add(w)
            for r in self.readers.get(k, ()):
                op.deps.add(r)
        for k in reads:
            self.readers.setdefault(k, []).append(op.idx)
        for k in writes:
            self.last_w[k] = op.idx
            self.readers[k] = []
        op.deps.discard(op.idx)
        self.ops.append(op)
        return op

    def emit(self, final_wait=True):
        nc = self.nc
        ops = self.ops
        for op in ops:
            for d in op.deps:
                dop = ops[d]
                if dop.dma:
                    continue
                if dop.eng == op.eng and not op.dma:
                    if dop.eng == "pe" or not self.same:
                        continue
                dop.need_inc = True
        cnt = {e: 0 for e in self.ENGS}
        dcount = {e: 0 for e in self.ENGS}
        for op in ops:
            if op.dma:
                op.cnt = dcount[op.eng]
                dcount[op.eng] += 1
            elif op.need_inc:
                cnt[op.eng] += 1
                op.cnt = cnt[op.eng]
        import contextlib
        stack = contextlib.ExitStack()
        sems = {}
        for e in self.ENGS:
            ngen = cnt[e] // GEN + 1
            sems[e] = [stack.enter_context(nc.semaphore(f"s_{e}_{g}_{id(self)%9973}")) for g in range(ngen)]
        dsems = {}
        for e in self.ENGS:
            n = min(NDMA, dcount[e])
            dsems[e] = [stack.enter_context(nc.semaphore(f"d_{e}_{i}_{id(self)%9973}")) for i in range(n)]
        slot_last = {}
        slot_val = {}
        for op in ops:
            if op.dma:
                s = op.cnt % NDMA
                key = (op.eng, s)
                op.dsem = dsems[op.eng][s]
                slot_val[key] = slot_val.get(key, 0) + 16
                op.dval = slot_val[key]
                op.prev_dma = slot_last.get(key)
                slot_last[key] = op
        per_eng = {e: [] for e in self.ENGS}
        for op in ops:
            per_eng[op.eng].append(op)
        last_dma_per_eng = {e: [o for o in per_eng[e] if o.dma] for e in self.ENGS}

        def run_engine(e, eng):
            seen = {}
            seend = {}

            def wait_c(dop):
                g, v = divmod(dop.cnt - 1, GEN)
                v += 1
                if seen.get((dop.eng, g), 0) >= v:
                    return
                eng.wait_ge(sems[dop.eng][g], v)
                seen[(dop.eng, g)] = v
                for gg in range(g):
                    seen[(dop.eng, gg)] = GEN

            def wait_d(dop):
                k = id(dop.dsem)
                if seend.get(k, 0) >= dop.dval:
                    return
                eng.wait_ge(dop.dsem, dop.dval)
                seend[k] = dop.dval

            for op in per_eng[e]:
                for d in sorted(op.deps):
                    dop = ops[d]
                    if dop.dma:
                        wait_d(dop)
                    else:
                        if dop.eng == e and not op.dma and (e == "pe" or not self.same):
                            continue
                        if dop.cnt is None:
                            continue
                        wait_c(dop)
                if op.dma and op.prev_dma is not None:
                    wait_d(op.prev_dma)
                ins = getattr(eng, op.fn[0])(*op.fn[1], **op.fn[2])
                if op.dma:
                    ins.then_inc(op.dsem, 16)
                elif op.need_inc:
                    g = (op.cnt - 1) // GEN
                    ins.then_inc(sems[e][g], 1)
            if final_wait:
                lastd = {}
                for o in last_dma_per_eng[e]:
                    lastd[id(o.dsem)] = o
                for o in lastd.values():
                    wait_d(o)

        with nc.Block() as block:
            if per_eng["sp"]:
                @block.sync
                def _(eng):
                    run_engine("sp", eng)
            if per_eng["act"]:
                @block.scalar
                def _(eng):
                    run_engine("act", eng)
            if per_eng["dve"]:
                @block.vector
                def _(eng):
                    run_engine("dve", eng)
            if per_eng["pool"]:
                @block.gpsimd
                def _(eng):
                    run_engine("pool", eng)
            if per_eng["pe"]:
                @block.tensor
                def _(eng):
                    run_engine("pe", eng)
        _SEM_STACKS.append(stack)


import numpy as np
import concourse.bass as bass
import concourse.mybir as mybir
from concourse.bass_utils import run_bass_kernel_spmd

F32 = mybir.dt.float32
BF16 = mybir.dt.bfloat16
I32 = mybir.dt.int32
AF = mybir.ActivationFunctionType
ALU = mybir.AluOpType
AX = mybir.AxisListType

NT = 17
TS = 17
NTT = 18
EPS = 1e-6
NPAGE = 64
SL = [2.0 ** (-2 * (h + 1)) for h in range(4)]


STOP_AFTER = 'all'
SKIP_SAMPLE_ATTN = False
NCORES = 8
CK_ROWS = 2560 * 128
DBG_TILES = NT
DBG_STAGES = 4
POOL_CONV = True
DBG_FFN_TILES = None


def build():
    nc = bass.Bass("TRN2", target_bir_lowering=False)
    def din(name, shape, dt=F32):
        return nc.dram_tensor(name, list(shape), dt, kind="ExternalInput").ap()
    def dout(name, shape):
        return nc.dram_tensor(name, list(shape), F32, kind="ExternalOutput").ap()
    xp = din("xp", [NT * 128, 1024]); xs4 = din("xs4", [128, 1024])
    ckv = din("ckv", [CK_ROWS, 1024])
    ptab = din("ptab", [1, 256], I32); state = din("state", [4, 30, 512])
    w_in = din("w_in", [1024, 2560]); w_out = din("w_out", [1024, 1024])
    w_gate = din("w_gate", [1024, 2816]); w_up = din("w_up", [1024, 2816]); w_down = din("w_down", [2816, 1024])
    g_pre = din("g_pre", [1, 1024]); g_post = din("g_post", [1, 1024]); g_fpre = din("g_fpre", [1, 1024]); g_fpost = din("g_fpost", [1, 1024])
    subln = din("subln", [1, 128]); conv_w = din("conv_w", [31, 512]); conv_b = din("conv_b", [1, 512])
    cln_w = din("cln_w", [1, 512]); cln_b = din("cln_b", [1, 512])
    lq1 = din("lq1", [1, 64]); lk1 = din("lk1", [1, 64]); lq2 = din("lq2", [1, 64]); lk2 = din("lk2", [1, 64])
    c_identf = din("c_identf", [128, 128]); c_mask = din("c_mask", [128, 128])
    c_qx = din("c_qx", [4, 8, NT * 128]); c_kx = din("c_kx", [4, 8, NT * 128])
    c_posT = din("c_posT", [2, 128]); c_abias = din("c_abias", [2, 65, 8]); c_mnew = din("c_mnew", [128, 4, 8]); c_dmask = din("c_dmask", [128, 8, 4])
    c_sel = din("c_sel", [31, 4, 4]); c_iota = din("c_iota", [128, 1])
    y_p = dout("y_p", [2048, 1024]); y_s = dout("y_s", [4, 1024])
    nk_p = dout("nk_p", [2064, 512]); nv_p = dout("nv_p", [2064, 512]); nc_p = dout("nc_p", [30, 512])
    nk_s = dout("nk_s", [4, 512]); nv_s = dout("nv_s", [4, 512]); nc_s = dout("nc_s", [4, 30, 512])

    import contextlib
    outer = contextlib.ExitStack()
    def sb(name, shape, dt=F32, st=None):
        return (st or outer).enter_context(nc.sbuf_tensor(name, list(shape), dt))
    def ps(name, shape, dt=F32, st=None):
        return (st or outer).enter_context(nc.psum_tensor(name, list(shape), dt))
    d_all = sb("d_all", [128, NTT, 1024], BF16)
    identf = sb("identf", [128, 128]); identb = sb("identb", [128, 128], BF16)
    eps_t = sb("eps_t", [128, 1])
    ptr = ps("ptr", [128, 8, 128], BF16)
    pj = [ps(f"pj{i}", [128, 512]) for i in range(2)]

    def xsrc(t):
        return xs4 if t == TS else xp[t * 128:(t + 1) * 128, :]

    def rstd_from_ss(S, ss_ap, n, out_ap, key_in, key_out):
        S.add("act", R.activation(out=out_ap, in_=ss_ap, func=AF.Ln, bias=eps_t[:, 0:1], scale=1.0 / n), reads=[key_in, "eps"], writes=[key_out])
        S.add("act", R.activation(out=out_ap, in_=out_ap, func=AF.Exp, scale=-0.5), reads=[key_out], writes=[key_out])

    st1 = contextlib.ExitStack()
    st1a = contextlib.ExitStack()
    S = Sched(nc)
    w_in_b = sb("w_in_b", [128, 8, 2560], BF16, st1); w_out_b = sb("w_out_b", [128, 8, 1024], BF16, st1)
    qT = sb("qT", [68, 8, 128], BF16, st1);
    gpre_bc = sb("gpre_bc", [128, 1024], F32, st1); gpost_bc = sb("gpost_bc", [128, 1024], F32, st1)
    subln_bc = sb("subln_bc", [128, 128], F32, st1); clnw_bc = sb("clnw_bc", [128, 512], F32, st1); clnb_bc = sb("clnb_bc", [128, 512], F32, st1)
    convb_bc = sb("convb_bc", [128, 512], F32, st1)
    convw_tm = sb("convw_tm", [31, 512], F32, st1); convwT = sb("convwT", [128, 4, 31], F32, st1)
    lamt = sb("lamt", [128, 8, 64], F32, st1); lam = sb("lam", [128, 4], F32, st1)
    maskb = sb("maskb", [128, 128], BF16, st1)
    xs = [sb("xs0", [128, 1024], F32, st1)] * 2
    junk = sb("junk", [128, 1024], BF16, st1)
    xn = sb("xn", [128, 1024], BF16, st1); xnT = sb("xnT", [128, 8, 128], BF16, st1)
    st = sb("stat", [128, 16], F32, st1)
    qk_tm = sb("qk_tm", [128, 1024], BF16, st1); kv_st = [sb("kv_st0", [128, 1024], F32, st1)] * 2
    u_tm = [sb("u_tm0", [128, 512], F32, st1)] * 2
    cat_tm = sb("cat_tm", [128, 1024], BF16, st1); catT = sb("catT", [128, 8, 128], BF16, st1)
    PT = [sb(f"PT{i}", [128, 512], BF16, st1) for i in range(2)]
    o_h = sb("o_h", [128, 128], F32, st1); t1 = sb("t1", [128, 128], F32, st1); ytmp = sb("ytmp", [128, 128], F32, st1)
    z_t = sb("z_t", [128, 512], F32, st1); z2 = sb("z2", [128, 512], F32, st1); e_t = z2
    iot = sb("iot", [128, 1], F32, st1); sel = sb("sel", [31, 4, 4], F32, st1)
    ptf = ps("ptf", [128, 512], F32, st1a)
    pst = [ps(f"pst{i}", [128, 512], F32, st1a) for i in range(2)]
    pO_ = [ps(f"pO{i}", [128, 512], F32, st1a) for i in range(2)]
    pO = [p_[:, 0:258].rearrange("p (j e) -> p j e", j=2) for p_ in pO_]
    kT = sb("kT", [68, 8, NT * 128], BF16, st1a); Vaug = sb("Vaug", [128, NT, 4, 129], BF16, st1a)
    uT = sb("uT", [128, 4, 158], F32, st1a); yT = sb("yT", [128, 4, 128], F32, st1a)

    S.add("sp", R.dma_start(out=identf[:], in_=c_identf), writes=["identf"], dma=True)
    S.add("dve", R.tensor_copy(out=identb[:], in_=identf[:]), reads=["identf"], writes=["identb"])
    S.add("dve", R.memset(eps_t[:], EPS), writes=["eps"])
    S.add("pool", R.dma_start(out=maskb[:], in_=c_mask), writes=["maskb"], dma=True)
    S.add("pool", R.dma_start(out=kT[64:68, :, :], in_=c_kx), writes=["kTx"], dma=True)
    for k in range(8):
        S.add("pool", R.dma_start(out=w_in_b[:, k, :], in_=w_in[k * 128:(k + 1) * 128, :]), writes=[f"w_in{k}"], dma=True)
    for k in range(8):
        S.add("pool", R.dma_start(out=w_out_b[:, k, :], in_=w_out[k * 128:(k + 1) * 128, :]), writes=[f"w_out{k}"], dma=True)
    for (t_, src, key) in ((gpre_bc, g_pre, "gpre"), (gpost_bc, g_post, "gpost"), (subln_bc, subln, "subln"), (clnw_bc, cln_w, "clnw"), (clnb_bc, cln_b, "clnb"), (convb_bc, conv_b, "convb")):
        S.add("sp", R.dma_start(out=t_[:], in_=src.partition_broadcast(128)), writes=[key], dma=True)
    S.add("dve", R.tensor_scalar(out=subln_bc[:], in0=subln_bc[:], scalar1=0.8, scalar2=None, op0=ALU.mult), reads=["subln"], writes=["subln"])
    S.add("sp", R.dma_start(out=convw_tm[:], in_=conv_w), writes=["convw_tm"], dma=True)
    S.add("sp", R.dma_start(out=sel[:], in_=c_sel), writes=["sel"], dma=True)
    S.add("sp", R.dma_start(out=iot[:], in_=c_iota), writes=["iot"], dma=True)
    for c in range(4):
        S.add("pe", R.transpose(out=ptf[:, c * 31:(c + 1) * 31], in_=convw_tm[:, c * 128:(c + 1) * 128], identity=identf[0:31, 0:31]), reads=["convw_tm", "identf"], writes=["ptf"])
    S.add("dve", R.tensor_copy(out=convwT[:].rearrange("p c j -> p (c j)"), in_=ptf[:, 0:124]), reads=["ptf"], writes=["convwT"])
    for i, v in enumerate((lq1, lk1, lq2, lk2)):
        S.add("sp", R.dma_start(out=lamt[:, i, :], in_=v.partition_broadcast(128)), writes=[f"lamt{i}"], dma=True)
    S.add("dve", R.tensor_tensor(out=lamt[:, 4, :], in0=lamt[:, 0, :], in1=lamt[:, 1, :], op=ALU.mult), reads=["lamt0", "lamt1"], writes=["lamt4"])
    S.add("dve", R.tensor_tensor(out=lamt[:, 5, :], in0=lamt[:, 2, :], in1=lamt[:, 3, :], op=ALU.mult), reads=["lamt2", "lamt3"], writes=["lamt5"])
    S.add("dve", R.reduce_sum(out=lam[:, 0:2], in_=lamt[:, 4:6, :], axis=AX.X), reads=["lamt4", "lamt5"], writes=["lam"])
    S.add("act", R.activation(out=lam[:, 0:2], in_=lam[:, 0:2], func=AF.Exp), reads=["lam"], writes=["lam"])
    S.add("dve", R.tensor_tensor(out=lam[:, 2:3], in0=lam[:, 1:2], in1=lam[:, 0:1], op=ALU.subtract), reads=["lam"], writes=["lam"])
    S.add("dve", R.tensor_scalar(out=lam[:, 3:4], in0=lam[:, 2:3], scalar1=-0.2, scalar2=None, op0=ALU.add), reads=["lam"], writes=["lam"])
    S.add("pool", R.memset(Vaug[:, :, :, 128:129], 1.0), writes=["Vones"])
    S.add("pool", R.memset(uT[:, :, 0:30], 0.0), writes=["uT"])

    def dense_in(t):
        pre_x(t)
        dense_proj(t)

    def pre_x(t):
        sl = 0
        x_t = xs[sl]
        S.add("sp", R.dma_start(out=x_t[:], in_=xsrc(t)), writes=[f"xs{sl}"], dma=True)
        S.add("act", R.activation(out=junk[:], in_=x_t[:], func=AF.Square, accum_out=st[:, 0:1]), reads=[f"xs{sl}"], writes=["junk", "st0"])
        rstd_from_ss(S, st[:, 0:1], 1024, st[:, 1:2], "st0", "st1")
        S.add("dve", R.scalar_tensor_tensor(out=xn[:], in0=x_t[:], scalar=st[:, 1:2], in1=gpre_bc[:], op0=ALU.mult, op1=ALU.mult), reads=[f"xs{sl}", "st1", "gpre"], writes=["xn"])
        for k in range(8):
            S.add("pe", R.transpose(out=ptr[:, k, :], in_=xn[:, k * 128:(k + 1) * 128], identity=identb[:]), reads=["xn", "identb"], writes=["ptr"])
        S.add("act", R.copy(out=xnT[:], in_=ptr[:]), reads=["ptr"], writes=["xnT"])

    def dense_proj(t):
        sl = 0
        wk = [f"w_in{k}" for k in range(8)]
        def proj(blk, pb):
            for k in range(8):
                S.add("pe", R.matmul(pj[pb][:], lhsT=xnT[:, k, :], rhs=w_in_b[:, k, blk * 512:(blk + 1) * 512], start=(k == 0), stop=(k == 7)), reads=["xnT"] + wk, writes=[f"pj{pb}", f"projblk{blk}"])
        kvs = kv_st[sl]
        proj(0, 0)
        S.add("act", R.mul(out=qk_tm[:, 0:512], in_=pj[0][:], mul=0.125), reads=["pj0"], writes=["qk_q"])
        proj(1, 1)
        S.add("act", R.copy(out=kvs[:, 0:512], in_=pj[1][:]), reads=["pj1"], writes=[f"kvs{sl}k"])
        S.add("act", R.copy(out=qk_tm[:, 512:1024], in_=pj[1][:]), reads=["pj1"], writes=["qk_k"])
        proj(2, 0)
        S.add("act", R.copy(out=kvs[:, 512:1024], in_=pj[0][:]), reads=["pj0"], writes=[f"kvs{sl}v"])
        if t == TS:
            S.add("act", R.copy(out=Vn[:], in_=pj[0][:]), reads=["pj0"], writes=["Vn"])
        else:
            S.add("act", R.copy(out=Vaug[:, t, :, 0:128], in_=pj[0][:].rearrange("p (h e) -> p h e", h=4)), reads=["pj0"], writes=[f"V{t}"])
        if t == TS:
            S.add("sp", R.dma_start(out=nk_s, in_=kvs[0:4, 0:512]), reads=[f"kvs{sl}k"], dma=True)
            S.add("sp", R.dma_start(out=nv_s, in_=kvs[0:4, 512:1024]), reads=[f"kvs{sl}v"], dma=True)
        else:
            nr = 16 if t == 16 else 128
            S.add("sp", R.dma_start(out=nk_p[t * 128:t * 128 + nr, :], in_=kvs[0:nr, 0:512]), reads=[f"kvs{sl}k"], dma=True)
            S.add("sp", R.dma_start(out=nv_p[t * 128:t * 128 + nr, :], in_=kvs[0:nr, 512:1024]), reads=[f"kvs{sl}v"], dma=True)
        proj(3, 1)
        proj(4, 0)
        ut = u_tm[sl]
        S.add("act", R.activation(out=e_t[:], in_=pj[0][:], func=AF.Exp, scale=-1.0), reads=["pj0"], writes=["z2"])
        S.add("dve", R.tensor_scalar(out=e_t[:], in0=e_t[:], scalar1=1.0, scalar2=None, op0=ALU.add), reads=["z2"], writes=["z2"])
        S.add("dve", R.reciprocal(out=e_t[:], in_=e_t[:]), reads=["z2"], writes=["z2"])
        S.add("dve", R.tensor_tensor(out=ut[:], in0=pj[1][:], in1=e_t[:], op=ALU.mult), reads=["pj1", "z2"], writes=[f"u_tm{sl}"])
        if t == 15:
            S.add("sp", R.dma_start(out=nc_p[0:14, :], in_=ut[114:128, :]), reads=[f"u_tm{sl}"], dma=True)
        if t == 16:
            S.add("sp", R.dma_start(out=nc_p[14:30, :], in_=ut[0:16, :]), reads=[f"u_tm{sl}"], dma=True)
        if t == TS:
            for c in range(8):
                S.add("pe", R.transpose(out=ptr[:, c, :], in_=qk_tm[:, c * 128:(c + 1) * 128], identity=identb[:]), reads=["qk_q", "qk_k", "identb"], writes=["ptr"])
            S.add("act", R.copy(out=qkTT[:], in_=ptr[:]), reads=["ptr"], writes=["qkTT"])
            return
        for i in range(8):
            S.add("pe", R.transpose(out=ptr[0:64, i, :], in_=qk_tm[:, i * 64:(i + 1) * 64], identity=identb[:]), reads=["qk_q", "identb"], writes=["ptr", "ptrq"])
        S.add("act", R.copy(out=qT[0:64, :, :], in_=ptr[0:64, :, :]), reads=["ptr"], writes=["qT"])
        S.add("pool", R.dma_start(out=qT[64:68, :, :], in_=c_qx[:, :, t * 128:(t + 1) * 128]), writes=["qTx"], dma=True)
        for i in range(8):
            S.add("pe", R.transpose(out=ptr[0:64, i, :], in_=qk_tm[:, 512 + i * 64:512 + (i + 1) * 64], identity=identb[:]), reads=["qk_k", "identb"], writes=["ptr", "ptrk"])
        S.add("act", R.copy(out=kT[0:64, :, t * 128:(t + 1) * 128], in_=ptr[0:64, :, :]), reads=["ptr"], writes=[f"kT{t}"])

    def post_attn(h, num0, num1, rs0, rs1, keys, n=128):
        S.add("dve", R.tensor_scalar(out=t1[0:n], in0=num1, scalar1=rs1, scalar2=lam[0:n, 3:4], op0=ALU.mult, op1=ALU.mult), reads=keys + ["lam"], writes=["t1"])
        S.add("dve", R.scalar_tensor_tensor(out=o_h[0:n], in0=num0, scalar=rs0, in1=t1[0:n], op0=ALU.mult, op1=ALU.add), reads=keys + ["t1"], writes=["o_h"])
        S.add("act", R.activation(out=junk[0:n, 0:128], in_=o_h[0:n], func=AF.Square, accum_out=st[0:n, 6:7]), reads=["o_h"], writes=["junk", "st6"])
        S.add("act", R.activation(out=st[0:n, 7:8], in_=st[0:n, 6:7], func=AF.Ln, bias=eps_t[0:n, 0:1], scale=1.0 / 128), reads=["st6", "eps"], writes=["st7"])
        S.add("act", R.activation(out=st[0:n, 7:8], in_=st[0:n, 7:8], func=AF.Exp, scale=-0.5), reads=["st7"], writes=["st7"])
        S.add("dve", R.scalar_tensor_tensor(out=cat_tm[0:n, h * 128:(h + 1) * 128], in0=o_h[0:n], scalar=st[0:n, 7:8], in1=subln_bc[0:n], op0=ALU.mult, op1=ALU.mult), reads=["o_h", "st7", "subln"], writes=[f"cat{h}"])

    def post_attn_prompt(h, pb):
        S.add("dve", R.reciprocal(out=st[:, 4:6], in_=pO[pb][:, :, 128]), reads=[f"pO{pb}"], writes=["st45"])
        post_attn(h, pO[pb][:, 0, 0:128], pO[pb][:, 1, 0:128], st[:, 4:5], st[:, 5:6], [f"pO{pb}", "st45"])

    cnt = {"pst": 0}
    def attn_head(qt, h):
        pb = h % 2
        for j in range(2):
            pr = 2 * h + j
            for g0 in range(0, qt + 1, 4):
                kts = list(range(g0, min(g0 + 4, qt + 1)))
                sb_ = cnt["pst"] % 2; cnt["pst"] += 1
                for i, kt in enumerate(kts):
                    o_ap = pst[sb_][:, i * 128:(i + 1) * 128]
                    if kt == qt:
                        S.add("pe", R.matmul(o_ap, lhsT=identb[:], rhs=maskb[:], start=True, stop=False), reads=["identb", "maskb"], writes=[f"pst{sb_}"])
                    S.add("pe", R.matmul(o_ap, lhsT=kT[0:68, pr, kt * 128:(kt + 1) * 128], rhs=qT[0:68, pr, :], start=(kt != qt), stop=True), reads=[f"kT{kt}", "kTx", "qT", "qTx"], writes=[f"pst{sb_}"])
                n = len(kts) * 128
                S.add("act", R.activation(out=PT[sb_][:, 0:n], in_=pst[sb_][:, 0:n], func=AF.Exp), reads=[f"pst{sb_}"], writes=[f"PT{sb_}"])
                for i, kt in enumerate(kts):
                    S.add("pe", R.matmul(pO[pb][:, j, :], lhsT=PT[sb_][:, i * 128:(i + 1) * 128], rhs=Vaug[:, kt, h, :], start=(kt == 0), stop=(kt == qt)), reads=[f"PT{sb_}", f"V{kt}", "Vones"], writes=[f"pO{pb}"])
        post_attn_prompt(h, pb)

    def conv_u(t):
        sl = 0
        for c in range(4):
            S.add("pe", R.transpose(out=ptf[:, c * 128:(c + 1) * 128], in_=u_tm[sl][:, c * 128:(c + 1) * 128], identity=identf[:]), reads=[f"u_tm{sl}", "identf"], writes=["ptf"])
        S.add("act", R.copy(out=uT[:, :, 30:158], in_=ptf[:].rearrange("p (c t) -> p c t", c=4)), reads=["ptf"], writes=["uT"])

    def conv_taps(j0, j1):
        for j in range(j0, j1):
            for c in range(4):
                if c == 3 and POOL_CONV:
                    if j == 0:
                        S.add("pool", R.tensor_scalar(out=yT[:, c, :], in0=uT[:, c, j:j + 128], scalar1=convwT[:, c, j:j + 1], scalar2=None, op0=ALU.mult), reads=["uT", "convwT"], writes=[f"yT{c}"])
                    else:
                        S.add("pool", R.tensor_scalar(out=ytmp[:], in0=uT[:, c, j:j + 128], scalar1=convwT[:, c, j:j + 1], scalar2=None, op0=ALU.mult), reads=["uT", "convwT"], writes=["ytmp"])
                        S.add("pool", R.tensor_tensor(out=yT[:, c, :], in0=yT[:, c, :], in1=ytmp[:], op=ALU.add), reads=["ytmp", f"yT{c}"], writes=[f"yT{c}"])
                    continue
                if j == 0:
                    S.add("dve", R.tensor_scalar(out=yT[:, c, :], in0=uT[:, c, j:j + 128], scalar1=convwT[:, c, j:j + 1], scalar2=None, op0=ALU.mult), reads=["uT", "convwT"], writes=[f"yT{c}"])
                else:
                    S.add("dve", R.scalar_tensor_tensor(out=yT[:, c, :], in0=uT[:, c, j:j + 128], scalar=convwT[:, c, j:j + 1], in1=yT[:, c, :], op0=ALU.mult, op1=ALU.add), reads=["uT", "convwT", f"yT{c}"], writes=[f"yT{c}"])

    def conv_tail(t):
        yk = [f"yT{c}" for c in range(4)]
        S.add("act", R.copy(out=uT[:, :, 0:30], in_=uT[:, :, 128:158]), reads=["uT"] + yk, writes=["uT"])
        for c in range(4):
            S.add("pe", R.transpose(out=ptf[:, c * 128:(c + 1) * 128], in_=yT[:, c, :], identity=identf[:]), reads=[f"yT{c}", "identf"], writes=["ptf"])

    def conv_out(src_ap, src_key, n=128):
        S.add("dve", R.tensor_tensor(out=z_t[0:n], in0=src_ap, in1=convb_bc[0:n], op=ALU.add), reads=[src_key, "convb"], writes=["z_t"])
        S.add("dve", R.reduce_sum(out=st[0:n, 8:9], in_=z_t[0:n], axis=AX.X), reads=["z_t"], writes=["st8"])
        S.add("act", R.activation(out=junk[0:n, 0:512], in_=z_t[0:n], func=AF.Square, accum_out=st[0:n, 9:10]), reads=["z_t"], writes=["junk", "st9"])
        S.add("dve", R.tensor_scalar(out=st[0:n, 8:9], in0=st[0:n, 8:9], scalar1=1.0 / 512, scalar2=None, op0=ALU.mult), reads=["st8"], writes=["st8"])
        S.add("dve", R.tensor_tensor(out=st[0:n, 10:11], in0=st[0:n, 8:9], in1=st[0:n, 8:9], op=ALU.mult), reads=["st8"], writes=["st10"])
        S.add("dve", R.scalar_tensor_tensor(out=st[0:n, 10:11], in0=st[0:n, 10:11], scalar=-512.0, in1=st[0:n, 9:10], op0=ALU.mult, op1=ALU.add), reads=["st10", "st9"], writes=["st10"])
        S.add("act", R.activation(out=st[0:n, 11:12], in_=st[0:n, 10:11], func=AF.Ln, bias=eps_t[0:n, 0:1], scale=1.0 / 512), reads=["st10", "eps"], writes=["st11"])
        S.add("act", R.activation(out=st[0:n, 11:12], in_=st[0:n, 11:12], func=AF.Exp, scale=-0.5), reads=["st11"], writes=["st11"])
        S.add("dve", R.tensor_scalar(out=z_t[0:n], in0=z_t[0:n], scalar1=st[0:n, 8:9], scalar2=st[0:n, 11:12], op0=ALU.subtract, op1=ALU.mult), reads=["z_t", "st8", "st11"], writes=["z_t"])
        S.add("dve", R.tensor_tensor(out=z_t[0:n], in0=z_t[0:n], in1=clnw_bc[0:n], op=ALU.mult), reads=["z_t", "clnw"], writes=["z_t"])
        S.add("dve", R.tensor_tensor(out=z_t[0:n], in0=z_t[0:n], in1=clnb_bc[0:n], op=ALU.add), reads=["z_t", "clnb"], writes=["z_t"])
        S.add("act", R.activation(out=z2[0:n], in_=z_t[0:n], func=AF.Exp, scale=-1.0), reads=["z_t"], writes=["z2"])
        S.add("dve", R.tensor_scalar(out=z2[0:n], in0=z2[0:n], scalar1=1.0, scalar2=None, op0=ALU.add), reads=["z2"], writes=["z2"])
        S.add("dve", R.reciprocal(out=z2[0:n], in_=z2[0:n]), reads=["z2"], writes=["z2"])
        S.add("dve", R.tensor_tensor(out=cat_tm[0:n, 512:1024], in0=z_t[0:n], in1=z2[0:n], op=ALU.mult), reads=["z_t", "z2"], writes=["cat4"])

    def dense_mid(t):
        ck_ = [f"cat{h}" for h in range(5)]
        for k in range(8):
            S.add("pe", R.transpose(out=ptr[:, k, :], in_=cat_tm[:, k * 128:(k + 1) * 128], identity=identb[:]), reads=ck_ + ["identb"], writes=["ptr"])
        S.add("act", R.copy(out=catT[:], in_=ptr[:]), reads=["ptr"], writes=["catT"])
        wk = [f"w_out{k}" for k in range(8)]
        for hf in range(2):
            for k in range(8):
                S.add("pe", R.matmul(pj[hf][:], lhsT=catT[:, k, :], rhs=w_out_b[:, k, hf * 512:(hf + 1) * 512], start=(k == 0), stop=(k == 7)), reads=["catT"] + wk, writes=[f"pj{hf}"])
            S.add("act", R.activation(out=junk[:, 0:512], in_=pj[hf][:], func=AF.Square, accum_out=st[:, 12 + hf:13 + hf]), reads=[f"pj{hf}"], writes=["junk", f"st{12 + hf}"])
        S.add("dve", R.tensor_tensor(out=st[:, 14:15], in0=st[:, 12:13], in1=st[:, 13:14], op=ALU.add), reads=["st12", "st13"], writes=["st14"])
        rstd_from_ss(S, st[:, 14:15], 1024, st[:, 15:16], "st14", "st15")
        for hf in range(2):
            S.add("dve", R.scalar_tensor_tensor(out=d_all[:, t, hf * 512:(hf + 1) * 512], in0=pj[hf][:], scalar=st[:, 15:16], in1=gpost_bc[:, hf * 512:(hf + 1) * 512], op0=ALU.mult, op1=ALU.mult), reads=[f"pj{hf}", "st15", "gpost"], writes=[f"d{t}"])

    parts = [(0, 8), (8, 16), (16, 24), (24, 31)]
    if DBG_TILES > 0:
        dense_in(0)
    for t in range(DBG_TILES):
        if t + 1 < DBG_TILES:
            pre_x(t + 1)
        conv_u(t)
        for h in range(4):
            conv_taps(*parts[h])
            attn_head(t, h)
        if t + 1 < DBG_TILES:
            dense_proj(t + 1)
        conv_tail(t)
        conv_out(ptf[:], "ptf")
        dense_mid(t)

    S.emit()
    st1a.close()
    if STOP_AFTER == '1a':
        st1.close(); outer.close(); _SEM_STACKS.clear(); return nc
    S = Sched(nc)
    st1b = contextlib.ExitStack()
    ptf = ps("ptf_b", [128, 512], F32, st1b)
    kTp = [ps(f"kTp{i}", [128, 8, 128], BF16, st1b) for i in range(2)]
    pss = [ps(f"pss{i}", [128, 512], F32, st1b) for i in range(2)]
    ptf32 = sb("ptf32", [128, 4, 64], F32, st1b); pti = sb("pti", [128, 4, 64], I32, st1b)
    idxf = sb("idxf", [128, 4, 64], F32, st1b); idxi = sb("idxi", [128, 4, 64], I32, st1b)
    NB = 3
    kvpg = [sb(f"kvpg{i}", [128, 1024], BF16, st1b) for i in range(NB)]
    kTs = [sb(f"kTs{i}", [128, 4, 128], BF16, st1b) for i in range(2)]
    Vn = sb("Vn", [128, 512], BF16, st1b); qkTT = sb("qkTT", [128, 8, 128], BF16, st1b)
    Qbd = [sb(f"Qbd{i}", [128, 4, 2], BF16, st1b) for i in range(4)]
    Ps = [[sb(f"Ps{i}_{b}", [128, 8, 4], BF16, st1b) for b in range(2)] for i in range(4)]
    posT = sb("posT", [2, 128], BF16, st1b); abias = sb("abias", [2, 65, 8], BF16, st1b)
    mnew = sb("mnew", [128, 4, 8], BF16, st1b); ones4 = sb("ones4", [128, 4], BF16, st1b)
    dmask = sb("dmask", [128, 8, 4], F32, st1b); ssum = sb("ssum", [128, 8, 4], F32, st1b); rsum = sb("rsum", [128, 8], F32, st1b)
    ufull = [sb(f"ufull{i}", [31, 512], F32, st1b) for i in range(2)]; prod = sb("prod", [31, 512], F32, st1b)
    S.add("pool", R.dma_start(out=posT[:], in_=c_posT), writes=["posT"], dma=True)
    S.add("pool", R.dma_start(out=abias[:], in_=c_abias), writes=["abias"], dma=True)
    S.add("pool", R.dma_start(out=mnew[:], in_=c_mnew), writes=["mnew"], dma=True)
    S.add("sp", R.dma_start(out=dmask[:], in_=c_dmask), writes=["dmask"], dma=True)
    S.add("pool", R.memset(ones4[:], 1.0), writes=["ones4"])
    S.add("pool", R.memset(cat_tm[:], 0.0), writes=["cat0", "cat1", "cat2", "cat3", "cat4"])
    for i in range(4):
        S.add("pool", R.memset(Qbd[i][:], 0.0), writes=[f"Qbd{i}"])
        for b in range(2):
            S.add("pool", R.memset(Ps[i][b][:], 0.0), writes=[f"Ps{i}_{b}"])
    dense_in(TS)
    sl = 0
    S.add("sp", R.dma_start(out=pti[:].rearrange("p s j -> p (s j)"), in_=ptab.partition_broadcast(128)), writes=["pti"], dma=True)
    S.add("dve", R.tensor_copy(out=ptf32[:], in_=pti[:]), reads=["pti"], writes=["ptf32"])
    S.add("dve", R.tensor_scalar(out=idxf[:], in0=ptf32[:], scalar1=128.0, scalar2=iot[:, 0:1], op0=ALU.mult, op1=ALU.add), reads=["ptf32", "iot"], writes=["idxf"])
    S.add("dve", R.tensor_copy(out=idxi[:], in_=idxf[:]), reads=["idxf"], writes=["idxi"])
    for s in range(4):
        S.add("dve", R.tensor_copy(out=Qbd[s][0:64, :, 0], in_=qkTT[0:64, 0:4, s]), reads=["qkTT", f"Qbd{s}"], writes=[f"Qbd{s}"])
        S.add("dve", R.tensor_copy(out=Qbd[s][64:128, :, 1], in_=qkTT[64:128, 0:4, s]), reads=["qkTT", f"Qbd{s}"], writes=[f"Qbd{s}"])
    accb = [pj[0][:].rearrange("p (j e) -> p j e", j=4), pj[1][:].rearrange("p (j e) -> p j e", j=4)]
    zeros4 = sb("zeros4", [128, 4], BF16, st1b)
    S.add("pool", R.memset(zeros4[:], 0.0), writes=["zeros4"])
    for b in range(2):
        S.add("pe", R.matmul(pj[b][0:4, :], lhsT=zeros4[:], rhs=w_out_b[:, 0, 0:512], start=True, stop=False), reads=["zeros4", "qkTT", "Vn"], writes=[f"pj{b}"])
    it = 0
    nsteps = 4 * (NPAGE + 1)
    for s in range(4):
        for jp in range(NPAGE + 1):
            b3 = it % NB; b2 = it % 2
            if jp < NPAGE:
                S.add("pool", R.indirect_dma_start(out=kvpg[b3][:], out_offset=None, in_=ckv, in_offset=bass.IndirectOffsetOnAxis(ap=idxi[:, s, jp:jp + 1], axis=0)), reads=["idxi"], writes=[f"kvpg{b3}"], dma=True)
                for c in range(4):
                    S.add("pe", R.transpose(out=kTp[b2][:, c, :], in_=kvpg[b3][:, c * 128:(c + 1) * 128], identity=identb[:]), reads=[f"kvpg{b3}"], writes=[f"kTp{b2}"])
                S.add("act", R.copy(out=kTs[b2][:], in_=kTp[b2][:, 0:4, :]), reads=[f"kTp{b2}"], writes=[f"kTs{b2}"])
                kt_ = lambda c: kTs[b2][:, c, :]
                kkeys = [f"kTs{b2}"]
                vt_ = kvpg[b3][:, 512:1024]; vkeys = [f"kvpg{b3}"]
            else:
                kt_ = lambda c: qkTT[:, 4 + c, :]
                kkeys = ["qkTT"]
                vt_ = Vn[:, :]; vkeys = ["Vn"]
            if jp < NPAGE:
                S.add("pe", R.matmul(pss[b2][:, 0:8], lhsT=posT[:, :], rhs=abias[:, jp, :], start=True, stop=False), reads=["posT", "abias"], writes=[f"pss{b2}"])
            else:
                S.add("pe", R.matmul(pss[b2][:, 0:8], lhsT=identb[:], rhs=mnew[:, s, :], start=True, stop=False), reads=["mnew"], writes=[f"pss{b2}"])
            for c in range(4):
                S.add("pe", R.matmul(pss[b2][:, 2 * c:2 * c + 2], lhsT=kt_(c), rhs=Qbd[s][:, c, :], start=False, stop=(c == 3)), reads=kkeys + [f"Qbd{s}"], writes=[f"pss{b2}"])
            S.add("act", R.activation(out=Ps[s][b2][:, :, s], in_=pss[b2][:, 0:8], func=AF.Exp), reads=[f"pss{b2}"], writes=[f"Ps{s}_{b2}"])
            first = (it == 0); last = (it == nsteps - 1)
            for pr in range(8):
                S.add("pe", R.matmul(accb[pr // 4][0:4, pr % 4, :], lhsT=Ps[s][b2][:, pr, :], rhs=vt_[:, (pr // 2) * 128:(pr // 2 + 1) * 128], start=False, stop=(last and pr % 4 == 3)), reads=[f"Ps{s}_{b2}"] + vkeys, writes=[f"pj{pr // 4}"])
            S.add("pe", R.matmul(ptf[0:4, 0:32], lhsT=ones4[:], rhs=Ps[s][b2][:].rearrange("p a b -> p (a b)"), start=first, stop=last), reads=[f"Ps{s}_{b2}", "ones4"], writes=["ptf"])
            it += 1
    S.add("dve", R.tensor_tensor(out=ssum[0:4], in0=ptf[0:4, 0:32].rearrange("p (a b) -> p a b", a=8), in1=dmask[0:4], op=ALU.mult), reads=["ptf", "dmask"], writes=["ssum"])
    S.add("dve", R.reduce_sum(out=rsum[0:4], in_=ssum[0:4], axis=AX.X), reads=["ssum"], writes=["rsum"])
    S.add("dve", R.reciprocal(out=rsum[0:4], in_=rsum[0:4]), reads=["rsum"], writes=["rsum"])
    for h in range(4):
        p0 = 2 * h; p1 = 2 * h + 1
        post_attn(h, accb[p0 // 4][0:4, p0 % 4, :], accb[p1 // 4][0:4, p1 % 4, :], rsum[0:4, p0:p0 + 1], rsum[0:4, p1:p1 + 1], [f"pj{p0 // 4}", "rsum"], n=4)
    for s in range(4):
        ub = ufull[s % 2]
        S.add("sp", R.dma_start(out=ub[0:30, :], in_=state[s]), writes=[f"ufull{s % 2}"], dma=True)
        S.add("sp", R.dma_start(out=ub[30:31, :], in_=u_tm[sl][s:s + 1, :]), reads=[f"u_tm{sl}"], writes=[f"ufull{s % 2}b"], dma=True)
        S.add("sp", R.dma_start(out=nc_s[s], in_=ub[1:31, :]), reads=[f"ufull{s % 2}", f"ufull{s % 2}b"], dma=True)
        S.add("dve", R.tensor_tensor(out=prod[:], in0=ub[:], in1=convw_tm[:], op=ALU.mult), reads=[f"ufull{s % 2}", f"ufull{s % 2}b", "convw_tm"], writes=["prod"])
        S.add("pe", R.matmul(ptf[0:4, :], lhsT=sel[:, s, :], rhs=prod[:], start=(s == 0), stop=(s == 3)), reads=["prod", "sel"], writes=["ptf"])
    conv_out(ptf[0:4, :], "ptf", n=4)
    dense_mid(TS)
    S.emit()
    st1b.close()
    st1.close()
    if STOP_AFTER == '1b':
        outer.close(); _SEM_STACKS.clear(); return nc

    st2 = contextlib.ExitStack()
    S = Sched(nc)
    wg = sb("wg", [128, 8, 2816], BF16, st2); wu = sb("wu", [128, 8, 2816], BF16, st2); wd = sb("wd", [128, 22, 1024], BF16, st2)
    gfpre_bc = sb("gfpre_bc", [128, 1024], F32, st2); gfpost_bc = sb("gfpost_bc", [128, 1024], F32, st2)
    xs2 = [sb(f"xm{i}", [128, 1024], F32, st2) for i in range(2)]
    junk2 = sb("junk2", [128, 1024], BF16, st2); xn2 = sb("xn2", [128, 1024], BF16, st2); hTs = [sb(f"hT{i}", [128, 8, 128], BF16, st2) for i in range(2)]
    actT = sb("actT", [128, 22, 128], BF16, st2); gsl = [sb(f"g2_{i}", [128, 512], F32, st2) for i in range(2)]
    st_ = sb("stat2", [128, 8], F32, st2); ob = [sb("ob0", [128, 1024], F32, st2)] * 2
    pf = [ps(f"pf{i}", [128, 512], F32, st2) for i in range(2)]
    pk = [ps(f"pk{i}", [128, 512], F32, st2) for i in range(2)]
    for k in range(8):
        S.add("pool", R.dma_start(out=wg[:, k, :], in_=w_gate[k * 128:(k + 1) * 128, :]), writes=[f"wg{k}"], dma=True)
        S.add("pool", R.dma_start(out=wu[:, k, :], in_=w_up[k * 128:(k + 1) * 128, :]), writes=[f"wu{k}"], dma=True)
    for f in range(22):
        S.add("pool", R.dma_start(out=wd[:, f, :], in_=w_down[f * 128:(f + 1) * 128, :]), writes=[f"wd{f}"], dma=True)
    S.add("sp", R.dma_start(out=gfpre_bc[:], in_=g_fpre.partition_broadcast(128)), writes=["gfpre"], dma=True)
    S.add("sp", R.dma_start(out=gfpost_bc[:], in_=g_fpost.partition_broadcast(128)), writes=["gfpost"], dma=True)
    wgk = [f"wg{k}" for k in range(8)]; wuk = [f"wu{k}" for k in range(8)]; wdk = [f"wd{f}" for f in range(22)]
    tiles = list(DBG_FFN_TILES if DBG_FFN_TILES is not None else range(NTT))

    def pre(t):
        sl = t % 2
        xm = xs2[sl]; h_t = hTs[sl]
        S.add("sp", R.dma_start(out=xm[:], in_=xsrc(t)), writes=[f"xm{sl}"], dma=True)
        S.add("dve", R.tensor_tensor(out=xm[:], in0=xm[:], in1=d_all[:, t, :], op=ALU.add), reads=[f"xm{sl}"], writes=[f"xm{sl}"])
        S.add("act", R.activation(out=junk2[:], in_=xm[:], func=AF.Square, accum_out=st_[:, 0:1]), reads=[f"xm{sl}"], writes=["junk2", "s0"])
        rstd_from_ss(S, st_[:, 0:1], 1024, st_[:, 1:2], "s0", "s1")
        S.add("dve", R.scalar_tensor_tensor(out=xn2[:], in0=xm[:], scalar=st_[:, 1:2], in1=gfpre_bc[:], op0=ALU.mult, op1=ALU.mult), reads=[f"xm{sl}", "s1", "gfpre"], writes=["xn2"])
        for k in range(8):
            S.add("pe", R.transpose(out=ptr[:, k, :], in_=xn2[:, k * 128:(k + 1) * 128], identity=identb[:]), reads=["xn2"], writes=["ptr"])
        S.add("act", R.copy(out=h_t[:], in_=ptr[:]), reads=["ptr"], writes=[f"hT{sl}"])

    def gateup(t):
        sl = t % 2
        h_t = hTs[sl]
        for gi, f0 in enumerate(range(0, 22, 4)):
            fs = list(range(f0, min(f0 + 4, 22)))
            n = len(fs) * 128
            pg, pu = (pj[0], pj[1]) if gi % 2 == 0 else (pk[0], pk[1])
            kg, ku = ("pj0", "pj1") if gi % 2 == 0 else ("pk0", "pk1")
            gs = gsl[gi % 2]
            for i, f in enumerate(fs):
                for k in range(8):
                    S.add("pe", R.matmul(pg[:, i * 128:(i + 1) * 128], lhsT=wg[:, k, f * 128:(f + 1) * 128], rhs=h_t[:, k, :], start=(k == 0), stop=(k == 7)), reads=[f"hT{sl}"] + wgk, writes=[kg])
            for i, f in enumerate(fs):
                for k in range(8):
                    S.add("pe", R.matmul(pu[:, i * 128:(i + 1) * 128], lhsT=wu[:, k, f * 128:(f + 1) * 128], rhs=h_t[:, k, :], start=(k == 0), stop=(k == 7)), reads=[f"hT{sl}"] + wuk, writes=[ku])
            S.add("act", R.activation(out=gs[:, 0:n], in_=pg[:, 0:n], func=AF.Silu), reads=[kg], writes=[f"g2_{gi % 2}"])
            S.add("dve", R.tensor_tensor(out=actT[:, f0:f0 + n // 128, :].rearrange("p f t -> p (f t)"), in0=pu[:, 0:n], in1=gs[:, 0:n], op=ALU.mult), reads=[ku, f"g2_{gi % 2}"], writes=[f"actT{f0}"])

    def down_post(t):
        sl = t % 2
        xm = xs2[sl]
        ak = [f"actT{f0}" for f0 in range(0, 22, 4)]
        for hf in range(2):
            for f in range(22):
                S.add("pe", R.matmul(pf[hf][:], lhsT=actT[:, f, :], rhs=wd[:, f, hf * 512:(hf + 1) * 512], start=(f == 0), stop=(f == 21)), reads=ak + wdk, writes=[f"pf{hf}"])
            S.add("act", R.activation(out=junk2[:, 0:512], in_=pf[hf][:], func=AF.Square, accum_out=st_[:, 2 + hf:3 + hf]), reads=[f"pf{hf}"], writes=["junk2", f"s{2 + hf}"])
        S.add("dve", R.tensor_tensor(out=st_[:, 4:5], in0=st_[:, 2:3], in1=st_[:, 3:4], op=ALU.add), reads=["s2", "s3"], writes=["s4"])
        rstd_from_ss(S, st_[:, 4:5], 1024, st_[:, 5:6], "s4", "s5")
        o_t = ob[0]
        for hf in range(2):
            S.add("dve", R.scalar_tensor_tensor(out=o_t[:, hf * 512:(hf + 1) * 512], in0=pf[hf][:], scalar=st_[:, 5:6], in1=gfpost_bc[:, hf * 512:(hf + 1) * 512], op0=ALU.mult, op1=ALU.mult), reads=[f"pf{hf}", "s5", "gfpost"], writes=["ob0"])
        S.add("dve", R.tensor_tensor(out=o_t[:], in0=o_t[:], in1=xm[:], op=ALU.add), reads=["ob0", f"xm{sl}"], writes=["ob0"])
        if t == 0:
            S.add("sp", R.dma_start(out=y_p[0:112, :], in_=o_t[16:128, :]), reads=["ob0"], dma=True)
        elif t < 16:
            S.add("sp", R.dma_start(out=y_p[t * 128 - 16:t * 128 + 112, :], in_=o_t[:, :]), reads=["ob0"], dma=True)
        elif t == 16:
            S.add("sp", R.dma_start(out=y_p[2032:2048, :], in_=o_t[0:16, :]), reads=["ob0"], dma=True)
        else:
            S.add("sp", R.dma_start(out=y_s, in_=o_t[0:4, :]), reads=["ob0"], dma=True)

    pre(tiles[0])
    for i, t in enumerate(tiles):
        gateup(t)
        if i + 1 < len(tiles):
            pre(tiles[i + 1])
        down_post(t)
    S.emit()
    st2.close()
    for s_ in _SEM_STACKS:
        s_.close()
    _SEM_STACKS.clear()
    outer.close()
    return nc


def _consts():
    c = {}
    c["c_identf"] = np.eye(128, dtype=np.float32)
    k = np.arange(128)[:, None]; q = np.arange(128)[None, :]
    c["c_mask"] = np.where(k > q, -30000.0, 0.0).astype(np.float32)
    npos = NT * 128
    pos = np.arange(npos); a = pos // 128; b = pos % 128
    qx = np.zeros((4, 8, npos), np.float32); kx = np.zeros((4, 8, npos), np.float32)
    for pr in range(8):
        s = SL[pr // 2]
        qx[0, pr] = -s * 128 * a; qx[1, pr] = -s * b; qx[2, pr] = 1; qx[3, pr] = 1
        kx[0, pr] = 1; kx[1, pr] = 1; kx[2, pr] = s * 128 * a; kx[3, pr] = s * b
    c["c_qx"] = qx; c["c_kx"] = kx
    posT = np.zeros((2, 128), np.float32); posT[0] = 1.0; posT[1] = np.arange(128)
    abias = np.zeros((2, 65, 8), np.float32)
    for pr in range(8):
        s = SL[pr // 2]
        abias[0, :64, pr] = -s * (8192 - 128 * np.arange(64)); abias[1, :64, pr] = s
    mnew = np.full((128, 4, 8), -30000.0, np.float32)
    dmask = np.zeros((128, 8, 4), np.float32)
    for s in range(4):
        mnew[s, s, :] = 0.0
        dmask[s, :, s] = 1.0
    c["c_posT"] = posT; c["c_abias"] = abias; c["c_mnew"] = mnew; c["c_dmask"] = dmask
    sel = np.zeros((31, 4, 4), np.float32)
    for s in range(4):
        sel[:, s, s] = 1.0
    c["c_sel"] = sel
    c["c_iota"] = np.arange(128, dtype=np.float32).reshape(128, 1)
    return c


def kernel(x_prompt, x_sample, cache_k, cache_v, state_conv, page_table, meta_tokens,
           ln_mix_pre, ln_mix_post, w_in, lambda_q1, lambda_k1, lambda_q2, lambda_k2,
           subln_w, conv_w, conv_b, conv_ln_w, conv_ln_b, w_out, ln_ffn_pre, ln_ffn_post,
           w_gate, w_up, w_down):
    f = lambda a: np.ascontiguousarray(np.asarray(a, dtype=np.float32))
    nc = build()
    consts = _consts()
    ckv = np.concatenate([f(cache_k).reshape(2560 * 128, 512)[:CK_ROWS], f(cache_v).reshape(2560 * 128, 512)[:CK_ROWS]], axis=1)
    shared = {
        "ckv": ckv, "w_in": f(w_in)[0], "w_out": f(w_out)[0], "w_gate": f(w_gate)[0], "w_up": f(w_up)[0], "w_down": f(w_down)[0],
        "g_pre": f(ln_mix_pre), "g_post": f(ln_mix_post), "g_fpre": f(ln_ffn_pre), "g_fpost": f(ln_ffn_post),
        "subln": f(subln_w), "conv_w": f(conv_w)[0], "conv_b": f(conv_b), "cln_w": f(conv_ln_w), "cln_b": f(conv_ln_b),
        "lq1": f(lambda_q1), "lk1": f(lambda_k1), "lq2": f(lambda_q2), "lk2": f(lambda_k2),
    }
    shared.update(consts)
    xpn = f(x_prompt); xsn = f(x_sample); meta = f(meta_tokens); stc = f(state_conv)[0]
    pt = np.ascontiguousarray(np.asarray(page_table, dtype=np.int32))
    in_maps = []
    for c in range(NCORES):
        xp = np.zeros((NT * 128, 1024), np.float32)
        xp[0:16] = meta; xp[16:2064] = xpn[c]
        xs4 = np.zeros((128, 1024), np.float32); xs4[0:4] = xsn[4 * c:4 * c + 4, 0]
        m = dict(shared)
        m.update({"xp": xp, "xs4": xs4, "ptab": np.ascontiguousarray(pt[4 * c:4 * c + 4]).reshape(1, 256), "state": np.ascontiguousarray(stc[4 * c:4 * c + 4])})
        in_maps.append(m)
    res = run_bass_kernel_spmd(nc, in_maps, core_ids=list(range(NCORES)))
    R = res.results
    y_p = np.stack([R[c]["y_p"] for c in range(NCORES)])
    y_s = np.concatenate([R[c]["y_s"] for c in range(NCORES)])[:, None, :]
    nk_p = np.stack([R[c]["nk_p"] for c in range(NCORES)]).reshape(1, NCORES, 2064, 8, 64)
    nv_p = np.stack([R[c]["nv_p"] for c in range(NCORES)]).reshape(1, NCORES, 2064, 4, 128)
    nc_p = np.stack([R[c]["nc_p"] for c in range(NCORES)]).reshape(1, NCORES, 30, 512)
    nk_s = np.concatenate([R[c]["nk_s"] for c in range(NCORES)]).reshape(1, 4 * NCORES, 1, 8, 64)
    nv_s = np.concatenate([R[c]["nv_s"] for c in range(NCORES)]).reshape(1, 4 * NCORES, 1, 4, 128)
    nc_s = np.concatenate([R[c]["nc_s"] for c in range(NCORES)]).reshape(1, 4 * NCORES, 30, 512)
    return (y_p.astype(np.float32), y_s.astype(np.float32), nk_p, nv_p, nc_p, nk_s, nv_s, nc_s)
```

```python
GEN = 4000
DBG_SKIP = set()
DBG_SKIP_METH = set()
_SEM_STACKS = []
NDMA = 12


class _Rec:
    def __getattr__(self, name):
        def f(*a, **k):
            return (name, a, k)
        return f
R = _Rec()

class Op:
    __slots__ = ("eng", "fn", "deps", "dma", "idx", "need_inc", "cnt", "dsem", "dval", "prev_dma")

    def __init__(self, eng, fn, dma):
        self.eng = eng
        self.fn = fn
        self.dma = dma
        self.deps = set()
        self.need_inc = False
        self.cnt = None
        self.dsem = None
        self.dval = None
        self.prev_dma = None


class Sched:
    ENGS = ("pe", "act", "dve", "pool", "sp")

    def __init__(self, nc, same_engine_sync=True):
        self.nc = nc
        self.ops = []
        self.last_w = {}
        self.readers = {}
        self.same = same_engine_sync

    def add(self, eng, fn, reads=(), writes=(), dma=False):
        if any(k in DBG_SKIP for k in writes) or (DBG_SKIP_METH and fn[0] in DBG_SKIP_METH):
            return None
        op = Op(eng, fn, dma)
        op.idx = len(self.ops)
        for k in reads:
            w = self.last_w.get(k)
            if w is not None:
                op.deps.add(w)
            if k[:2] in ("pj", "pt", "ps", "pO", "pf"):
                for r in self.readers.get(k, ()):
                    op.deps.add(r)
        for k in writes:
            w = self.last_w.get(k)
            if w is not None:
                op.deps.add(w)
            for r in self.readers.get(k, ()):
                op.deps.add(r)
        for k in reads:
            self.readers.setdefault(k, []).append(op.idx)
        for k in writes:
            self.last_w[k] = op.idx
            self.readers[k] = []
        op.deps.discard(op.idx)
        self.ops.append(op)
        return op

    def emit(self, final_wait=True):
        nc = self.nc
        ops = self.ops
        for op in ops:
            for d in op.deps:
                dop = ops[d]
                if dop.dma:
                    continue
                if dop.eng == op.eng and not op.dma:
                    if dop.eng == "pe" or not self.same:
                        continue
                dop.need_inc = True
        cnt = {e: 0 for e in self.ENGS}
        dcount = {e: 0 for e in self.ENGS}
        for op in ops:
            if op.dma:
                op.cnt = dcount[op.eng]
                dcount[op.eng] += 1
            elif op.need_inc:
                cnt[op.eng] += 1
                op.cnt = cnt[op.eng]
        import contextlib
        stack = contextlib.ExitStack()
        sems = {}
        for e in self.ENGS:
            ngen = cnt[e] // GEN + 1
            sems[e] = [stack.enter_context(nc.semaphore(f"s_{e}_{g}_{id(self)%9973}")) for g in range(ngen)]
        dsems = {}
        for e in self.ENGS:
            n = min(NDMA, dcount[e])
            dsems[e] = [stack.enter_context(nc.semaphore(f"d_{e}_{i}_{id(self)%9973}")) for i in range(n)]
        slot_last = {}
        slot_val = {}
        for op in ops:
            if op.dma:
                s = op.cnt % NDMA
                key = (op.eng, s)
                op.dsem = dsems[op.eng][s]
                slot_val[key] = slot_val.get(key, 0) + 16
                op.dval = slot_val[key]
                op.prev_dma = slot_last.get(key)
                slot_last[key] = op
        per_eng = {e: [] for e in self.ENGS}
        for op in ops:
            per_eng[op.eng].append(op)
        last_dma_per_eng = {e: [o for o in per_eng[e] if o.dma] for e in self.ENGS}

        def run_engine(e, eng):
            seen = {}
            seend = {}

            def wait_c(dop):
                g, v = divmod(dop.cnt - 1, GEN)
                v += 1
                if seen.get((dop.eng, g), 0) >= v:
                    return
                eng.wait_ge(sems[dop.eng][g], v)
                seen[(dop.eng, g)] = v
                for gg in range(g):
                    seen[(dop.eng, gg)] = GEN

            def wait_d(dop):
                k = id(dop.dsem)
                if seend.get(k, 0) >= dop.dval:
                    return
                eng.wait_ge(dop.dsem, dop.dval)
                seend[k] = dop.dval

            for op in per_eng[e]:
                for d in sorted(op.deps):
                    dop = ops[d]
                    if dop.dma:
                        wait_d(dop)
                    else:
                        if dop.eng == e and not op.dma and (e == "pe" or not self.same):
                            continue
                        if dop.cnt is None:
                            continue
                        wait_c(dop)
                if op.dma and op.prev_dma is not None:
                    wait_d(op.prev_dma)
                ins = getattr(eng, op.fn[0])(*op.fn[1], **op.fn[2])
                if op.dma:
                    ins.then_inc(op.dsem, 16)
                elif op.need_inc:
                    g = (op.cnt - 1) // GEN
                    ins.then_inc(sems[e][g], 1)
            if final_wait:
                lastd = {}
                for o in last_dma_per_eng[e]:
                    lastd[id(o.dsem)] = o
                for o in lastd.values():
                    wait_d(o)

        with nc.Block() as block:
            if per_eng["sp"]:
                @block.sync
                def _(eng):
                    run_engine("sp", eng)
            if per_eng["act"]:
                @block.scalar
                def _(eng):
                    run_engine("act", eng)
            if per_eng["dve"]:
                @block.vector
                def _(eng):
                    run_engine("dve", eng)
            if per_eng["pool"]:
                @block.gpsimd
                def _(eng):
                    run_engine("pool", eng)
            if per_eng["pe"]:
                @block.tensor
                def _(eng):
                    run_engine("pe", eng)
        _SEM_STACKS.append(stack)


import numpy as np
import concourse.bass as bass
import concourse.mybir as mybir
from concourse.bass_utils import run_bass_kernel_spmd

F32 = mybir.dt.float32
BF16 = mybir.dt.bfloat16
I32 = mybir.dt.int32
AF = mybir.ActivationFunctionType
ALU = mybir.AluOpType
AX = mybir.AxisListType

NT = 17
TS = 17
NTT = 18
EPS = 1e-6
NPAGE = 64
SL = [2.0 ** (-2 * (h + 1)) for h in range(4)]


STOP_AFTER = 'all'
SKIP_SAMPLE_ATTN = False
NCORES = 8
CK_ROWS = 2560 * 128
DBG_TILES = NT
DBG_STAGES = 4
DBG_FFN_TILES = None


def build():
    nc = bass.Bass("TRN2", target_bir_lowering=False)
    def din(name, shape, dt=F32):
        return nc.dram_tensor(name, list(shape), dt, kind="ExternalInput").ap()
    def dout(name, shape):
        return nc.dram_tensor(name, list(shape), F32, kind="ExternalOutput").ap()
    xp = din("xp", [NT * 128, 1024]); xs4 = din("xs4", [128, 1024])
    ck = din("ck", [CK_ROWS, 512]); cv = din("cv", [CK_ROWS, 512])
    ptab = din("ptab", [1, 256], I32); state = din("state", [4, 30, 512])
    w_in = din("w_in", [1024, 2560]); w_out = din("w_out", [1024, 1024])
    w_gate = din("w_gate", [1024, 2816]); w_up = din("w_up", [1024, 2816]); w_down = din("w_down", [2816, 1024])
    g_pre = din("g_pre", [1, 1024]); g_post = din("g_post", [1, 1024]); g_fpre = din("g_fpre", [1, 1024]); g_fpost = din("g_fpost", [1, 1024])
    subln = din("subln", [1, 128]); conv_w = din("conv_w", [31, 512]); conv_b = din("conv_b", [1, 512])
    cln_w = din("cln_w", [1, 512]); cln_b = din("cln_b", [1, 512])
    lq1 = din("lq1", [1, 64]); lk1 = din("lk1", [1, 64]); lq2 = din("lq2", [1, 64]); lk2 = din("lk2", [1, 64])
    c_identf = din("c_identf", [128, 128]); c_mask = din("c_mask", [128, 128])
    c_qx = din("c_qx", [4, 8, NT * 128]); c_kx = din("c_kx", [4, 8, NT * 128])
    c_posT = din("c_posT", [2, 128]); c_abias = din("c_abias", [2, 65, 8]); c_mnew = din("c_mnew", [128, 4, 8]); c_dmask = din("c_dmask", [128, 8, 4]); c_shift = din("c_shift", [8, 4])
    c_sel = din("c_sel", [31, 4, 4]); c_iota = din("c_iota", [128, 1])
    y_p = dout("y_p", [2048, 1024]); y_s = dout("y_s", [4, 1024])
    nk_p = dout("nk_p", [2064, 512]); nv_p = dout("nv_p", [2064, 512]); nc_p = dout("nc_p", [30, 512])
    nk_s = dout("nk_s", [4, 512]); nv_s = dout("nv_s", [4, 512]); nc_s = dout("nc_s", [4, 30, 512])

    import contextlib
    outer = contextlib.ExitStack()
    def sb(name, shape, dt=F32, st=None):
        return (st or outer).enter_context(nc.sbuf_tensor(name, list(shape), dt))
    def ps(name, shape, dt=F32, st=None):
        return (st or outer).enter_context(nc.psum_tensor(name, list(shape), dt))
    d_all = sb("d_all", [128, NTT, 1024], BF16)
    identf = sb("identf", [128, 128]); identb = sb("identb", [128, 128], BF16)
    eps_t = sb("eps_t", [128, 1])
    ptr = ps("ptr", [128, 8, 128], BF16)
    pj = [ps(f"pj{i}", [128, 512]) for i in range(2)]

    def xsrc(t):
        return xs4 if t == TS else xp[t * 128:(t + 1) * 128, :]

    def rstd_from_ss(S, ss_ap, n, out_ap, key_in, key_out):
        S.add("act", R.activation(out=out_ap, in_=ss_ap, func=AF.Ln, bias=eps_t[:, 0:1], scale=1.0 / n), reads=[key_in, "eps"], writes=[key_out])
        S.add("act", R.activation(out=out_ap, in_=out_ap, func=AF.Exp, scale=-0.5), reads=[key_out], writes=[key_out])

    st1 = contextlib.ExitStack()
    st1a = contextlib.ExitStack()
    S = Sched(nc)
    w_in_b = sb("w_in_b", [128, 8, 2560], BF16, st1); w_out_b = sb("w_out_b", [128, 8, 1024], BF16, st1)
    qT = sb("qT", [68, 8, 128], BF16, st1);
    gpre_bc = sb("gpre_bc", [128, 1024], F32, st1); gpost_bc = sb("gpost_bc", [128, 1024], F32, st1)
    subln_bc = sb("subln_bc", [128, 128], F32, st1); clnw_bc = sb("clnw_bc", [128, 512], F32, st1); clnb_bc = sb("clnb_bc", [128, 512], F32, st1)
    convb_bc = sb("convb_bc", [128, 512], F32, st1)
    convw_tm = sb("convw_tm", [31, 512], F32, st1); convwT = sb("convwT", [128, 4, 31], F32, st1)
    lamt = sb("lamt", [128, 8, 64], F32, st1); lam = sb("lam", [128, 4], F32, st1)
    maskb = sb("maskb", [128, 128], BF16, st1)
    xs = [sb("xs0", [128, 1024], F32, st1)] * 2
    junk = sb("junk", [128, 1024], BF16, st1)
    xn = sb("xn", [128, 1024], BF16, st1); xnT = sb("xnT", [128, 8, 128], BF16, st1)
    st = sb("stat", [128, 16], F32, st1)
    qk_tm = sb("qk_tm", [128, 1024], BF16, st1); kv_st = [sb("kv_st0", [128, 1024], F32, st1)] * 2
    u_tm = [sb("u_tm0", [128, 512], F32, st1)] * 2
    cat_tm = sb("cat_tm", [128, 1024], BF16, st1); catT = sb("catT", [128, 8, 128], BF16, st1)
    PT = [sb(f"PT{i}", [128, 512], BF16, st1) for i in range(2)]
    o_h = sb("o_h", [128, 128], F32, st1); t1 = sb("t1", [128, 128], F32, st1)
    z_t = sb("z_t", [128, 512], F32, st1); z2 = sb("z2", [128, 512], F32, st1); e_t = z2
    iot = sb("iot", [128, 1], F32, st1); sel = sb("sel", [31, 4, 4], F32, st1)
    ptf = ps("ptf", [128, 512], F32, st1a)
    pst = [ps(f"pst{i}", [128, 512], F32, st1a) for i in range(2)]
    pO_ = [ps(f"pO{i}", [128, 512], F32, st1a) for i in range(2)]
    pO = [p_[:, 0:258].rearrange("p (j e) -> p j e", j=2) for p_ in pO_]
    kT = sb("kT", [68, 8, NT * 128], BF16, st1a); Vaug = sb("Vaug", [128, NT, 4, 129], BF16, st1a)
    uT = sb("uT", [128, 4, 158], F32, st1a); yT = sb("yT", [128, 4, 128], F32, st1a)

    S.add("sp", R.dma_start(out=identf[:], in_=c_identf), writes=["identf"], dma=True)
    S.add("dve", R.tensor_copy(out=identb[:], in_=identf[:]), reads=["identf"], writes=["identb"])
    S.add("dve", R.memset(eps_t[:], EPS), writes=["eps"])
    S.add("pool", R.dma_start(out=maskb[:], in_=c_mask), writes=["maskb"], dma=True)
    S.add("pool", R.dma_start(out=kT[64:68, :, :], in_=c_kx), writes=["kTx"], dma=True)
    for k in range(8):
        S.add("pool", R.dma_start(out=w_in_b[:, k, :], in_=w_in[k * 128:(k + 1) * 128, :]), writes=[f"w_in{k}"], dma=True)
    for k in range(8):
        S.add("pool", R.dma_start(out=w_out_b[:, k, :], in_=w_out[k * 128:(k + 1) * 128, :]), writes=[f"w_out{k}"], dma=True)
    for (t_, src, key) in ((gpre_bc, g_pre, "gpre"), (gpost_bc, g_post, "gpost"), (subln_bc, subln, "subln"), (clnw_bc, cln_w, "clnw"), (clnb_bc, cln_b, "clnb"), (convb_bc, conv_b, "convb")):
        S.add("sp", R.dma_start(out=t_[:], in_=src.partition_broadcast(128)), writes=[key], dma=True)
    S.add("dve", R.tensor_scalar(out=subln_bc[:], in0=subln_bc[:], scalar1=0.8, scalar2=None, op0=ALU.mult), reads=["subln"], writes=["subln"])
    S.add("sp", R.dma_start(out=convw_tm[:], in_=conv_w), writes=["convw_tm"], dma=True)
    S.add("sp", R.dma_start(out=sel[:], in_=c_sel), writes=["sel"], dma=True)
    S.add("sp", R.dma_start(out=iot[:], in_=c_iota), writes=["iot"], dma=True)
    for c in range(4):
        S.add("pe", R.transpose(out=ptf[:, c * 31:(c + 1) * 31], in_=convw_tm[:, c * 128:(c + 1) * 128], identity=identf[0:31, 0:31]), reads=["convw_tm", "identf"], writes=["ptf"])
    S.add("dve", R.tensor_copy(out=convwT[:].rearrange("p c j -> p (c j)"), in_=ptf[:, 0:124]), reads=["ptf"], writes=["convwT"])
    for i, v in enumerate((lq1, lk1, lq2, lk2)):
        S.add("sp", R.dma_start(out=lamt[:, i, :], in_=v.partition_broadcast(128)), writes=[f"lamt{i}"], dma=True)
    S.add("dve", R.tensor_tensor(out=lamt[:, 4, :], in0=lamt[:, 0, :], in1=lamt[:, 1, :], op=ALU.mult), reads=["lamt0", "lamt1"], writes=["lamt4"])
    S.add("dve", R.tensor_tensor(out=lamt[:, 5, :], in0=lamt[:, 2, :], in1=lamt[:, 3, :], op=ALU.mult), reads=["lamt2", "lamt3"], writes=["lamt5"])
    S.add("dve", R.reduce_sum(out=lam[:, 0:2], in_=lamt[:, 4:6, :], axis=AX.X), reads=["lamt4", "lamt5"], writes=["lam"])
    S.add("act", R.activation(out=lam[:, 0:2], in_=lam[:, 0:2], func=AF.Exp), reads=["lam"], writes=["lam"])
    S.add("dve", R.tensor_tensor(out=lam[:, 2:3], in0=lam[:, 1:2], in1=lam[:, 0:1], op=ALU.subtract), reads=["lam"], writes=["lam"])
    S.add("dve", R.tensor_scalar(out=lam[:, 3:4], in0=lam[:, 2:3], scalar1=-0.2, scalar2=None, op0=ALU.add), reads=["lam"], writes=["lam"])
    S.add("pool", R.memset(Vaug[:, :, :, 128:129], 1.0), writes=["Vones"])
    S.add("pool", R.memset(uT[:, :, 0:30], 0.0), writes=["uT"])

    def dense_in(t):
        pre_x(t)
        dense_proj(t)

    def pre_x(t):
        sl = 0
        x_t = xs[sl]
        S.add("sp", R.dma_start(out=x_t[:], in_=xsrc(t)), writes=[f"xs{sl}"], dma=True)
        S.add("act", R.activation(out=junk[:], in_=x_t[:], func=AF.Square, accum_out=st[:, 0:1]), reads=[f"xs{sl}"], writes=["junk", "st0"])
        rstd_from_ss(S, st[:, 0:1], 1024, st[:, 1:2], "st0", "st1")
        S.add("dve", R.scalar_tensor_tensor(out=xn[:], in0=x_t[:], scalar=st[:, 1:2], in1=gpre_bc[:], op0=ALU.mult, op1=ALU.mult), reads=[f"xs{sl}", "st1", "gpre"], writes=["xn"])
        for k in range(8):
            S.add("pe", R.transpose(out=ptr[:, k, :], in_=xn[:, k * 128:(k + 1) * 128], identity=identb[:]), reads=["xn", "identb"], writes=["ptr"])
        S.add("act", R.copy(out=xnT[:], in_=ptr[:]), reads=["ptr"], writes=["xnT"])

    def dense_proj(t):
        sl = 0
        wk = [f"w_in{k}" for k in range(8)]
        def proj(blk, pb):
            for k in range(8):
                S.add("pe", R.matmul(pj[pb][:], lhsT=xnT[:, k, :], rhs=w_in_b[:, k, blk * 512:(blk + 1) * 512], start=(k == 0), stop=(k == 7)), reads=["xnT"] + wk, writes=[f"pj{pb}", f"projblk{blk}"])
        kvs = kv_st[sl]
        proj(0, 0)
        S.add("act", R.mul(out=qk_tm[:, 0:512], in_=pj[0][:], mul=0.125), reads=["pj0"], writes=["qk_q"])
        proj(1, 1)
        S.add("act", R.copy(out=kvs[:, 0:512], in_=pj[1][:]), reads=["pj1"], writes=[f"kvs{sl}k"])
        S.add("act", R.copy(out=qk_tm[:, 512:1024], in_=pj[1][:]), reads=["pj1"], writes=["qk_k"])
        proj(2, 0)
        S.add("act", R.copy(out=kvs[:, 512:1024], in_=pj[0][:]), reads=["pj0"], writes=[f"kvs{sl}v"])
        if t == TS:
            S.add("act", R.copy(out=Vn[:], in_=pj[0][:]), reads=["pj0"], writes=["Vn"])
        else:
            S.add("act", R.copy(out=Vaug[:, t, :, 0:128], in_=pj[0][:].rearrange("p (h e) -> p h e", h=4)), reads=["pj0"], writes=[f"V{t}"])
        if t == TS:
            S.add("sp", R.dma_start(out=nk_s, in_=kvs[0:4, 0:512]), reads=[f"kvs{sl}k"], dma=True)
            S.add("sp", R.dma_start(out=nv_s, in_=kvs[0:4, 512:1024]), reads=[f"kvs{sl}v"], dma=True)
        else:
            nr = 16 if t == 16 else 128
            S.add("sp", R.dma_start(out=nk_p[t * 128:t * 128 + nr, :], in_=kvs[0:nr, 0:512]), reads=[f"kvs{sl}k"], dma=True)
            S.add("sp", R.dma_start(out=nv_p[t * 128:t * 128 + nr, :], in_=kvs[0:nr, 512:1024]), reads=[f"kvs{sl}v"], dma=True)
        proj(3, 1)
        proj(4, 0)
        ut = u_tm[sl]
        S.add("act", R.activation(out=e_t[:], in_=pj[0][:], func=AF.Exp, scale=-1.0), reads=["pj0"], writes=["z2"])
        S.add("dve", R.tensor_scalar(out=e_t[:], in0=e_t[:], scalar1=1.0, scalar2=None, op0=ALU.add), reads=["z2"], writes=["z2"])
        S.add("dve", R.reciprocal(out=e_t[:], in_=e_t[:]), reads=["z2"], writes=["z2"])
        S.add("dve", R.tensor_tensor(out=ut[:], in0=pj[1][:], in1=e_t[:], op=ALU.mult), reads=["pj1", "z2"], writes=[f"u_tm{sl}"])
        if t == 15:
            S.add("sp", R.dma_start(out=nc_p[0:14, :], in_=ut[114:128, :]), reads=[f"u_tm{sl}"], dma=True)
        if t == 16:
            S.add("sp", R.dma_start(out=nc_p[14:30, :], in_=ut[0:16, :]), reads=[f"u_tm{sl}"], dma=True)
        if t == TS:
            for c in range(8):
                S.add("pe", R.transpose(out=ptr[:, c, :], in_=qk_tm[:, c * 128:(c + 1) * 128], identity=identb[:]), reads=["qk_q", "qk_k", "identb"], writes=["ptr"])
            S.add("act", R.copy(out=qkTT[:], in_=ptr[:]), reads=["ptr"], writes=["qkTT"])
            return
        for i in range(8):
            S.add("pe", R.transpose(out=ptr[0:64, i, :], in_=qk_tm[:, i * 64:(i + 1) * 64], identity=identb[:]), reads=["qk_q", "identb"], writes=["ptr", "ptrq"])
        S.add("act", R.copy(out=qT[0:64, :, :], in_=ptr[0:64, :, :]), reads=["ptr"], writes=["qT"])
        S.add("pool", R.dma_start(out=qT[64:68, :, :], in_=c_qx[:, :, t * 128:(t + 1) * 128]), writes=["qTx"], dma=True)
        for i in range(8):
            S.add("pe", R.transpose(out=ptr[0:64, i, :], in_=qk_tm[:, 512 + i * 64:512 + (i + 1) * 64], identity=identb[:]), reads=["qk_k", "identb"], writes=["ptr", "ptrk"])
        S.add("act", R.copy(out=kT[0:64, :, t * 128:(t + 1) * 128], in_=ptr[0:64, :, :]), reads=["ptr"], writes=[f"kT{t}"])

    def post_attn(h, num0, num1, rs0, rs1, keys, n=128):
        S.add("dve", R.tensor_scalar(out=t1[0:n], in0=num1, scalar1=rs1, scalar2=lam[0:n, 3:4], op0=ALU.mult, op1=ALU.mult), reads=keys + ["lam"], writes=["t1"])
        S.add("dve", R.scalar_tensor_tensor(out=o_h[0:n], in0=num0, scalar=rs0, in1=t1[0:n], op0=ALU.mult, op1=ALU.add), reads=keys + ["t1"], writes=["o_h"])
        S.add("act", R.activation(out=junk[0:n, 0:128], in_=o_h[0:n], func=AF.Square, accum_out=st[0:n, 6:7]), reads=["o_h"], writes=["junk", "st6"])
        S.add("act", R.activation(out=st[0:n, 7:8], in_=st[0:n, 6:7], func=AF.Ln, bias=eps_t[0:n, 0:1], scale=1.0 / 128), reads=["st6", "eps"], writes=["st7"])
        S.add("act", R.activation(out=st[0:n, 7:8], in_=st[0:n, 7:8], func=AF.Exp, scale=-0.5), reads=["st7"], writes=["st7"])
        S.add("dve", R.scalar_tensor_tensor(out=cat_tm[0:n, h * 128:(h + 1) * 128], in0=o_h[0:n], scalar=st[0:n, 7:8], in1=subln_bc[0:n], op0=ALU.mult, op1=ALU.mult), reads=["o_h", "st7", "subln"], writes=[f"cat{h}"])

    def post_attn_prompt(h, pb):
        S.add("dve", R.reciprocal(out=st[:, 4:6], in_=pO[pb][:, :, 128]), reads=[f"pO{pb}"], writes=["st45"])
        post_attn(h, pO[pb][:, 0, 0:128], pO[pb][:, 1, 0:128], st[:, 4:5], st[:, 5:6], [f"pO{pb}", "st45"])

    cnt = {"pst": 0}
    def attn_head(qt, h):
        pb = h % 2
        for j in range(2):
            pr = 2 * h + j
            for g0 in range(0, qt + 1, 4):
                kts = list(range(g0, min(g0 + 4, qt + 1)))
                sb_ = cnt["pst"] % 2; cnt["pst"] += 1
                for i, kt in enumerate(kts):
                    o_ap = pst[sb_][:, i * 128:(i + 1) * 128]
                    if kt == qt:
                        S.add("pe", R.matmul(o_ap, lhsT=identb[:], rhs=maskb[:], start=True, stop=False), reads=["identb", "maskb"], writes=[f"pst{sb_}"])
                    S.add("pe", R.matmul(o_ap, lhsT=kT[0:68, pr, kt * 128:(kt + 1) * 128], rhs=qT[0:68, pr, :], start=(kt != qt), stop=True), reads=[f"kT{kt}", "kTx", "qT", "qTx"], writes=[f"pst{sb_}"])
                n = len(kts) * 128
                S.add("act", R.activation(out=PT[sb_][:, 0:n], in_=pst[sb_][:, 0:n], func=AF.Exp), reads=[f"pst{sb_}"], writes=[f"PT{sb_}"])
                for i, kt in enumerate(kts):
                    S.add("pe", R.matmul(pO[pb][:, j, :], lhsT=PT[sb_][:, i * 128:(i + 1) * 128], rhs=Vaug[:, kt, h, :], start=(kt == 0), stop=(kt == qt)), reads=[f"PT{sb_}", f"V{kt}", "Vones"], writes=[f"pO{pb}"])
        post_attn_prompt(h, pb)

    def conv_u(t):
        sl = 0
        for c in range(4):
            S.add("pe", R.transpose(out=ptf[:, c * 128:(c + 1) * 128], in_=u_tm[sl][:, c * 128:(c + 1) * 128], identity=identf[:]), reads=[f"u_tm{sl}", "identf"], writes=["ptf"])
        S.add("act", R.copy(out=uT[:, :, 30:158], in_=ptf[:].rearrange("p (c t) -> p c t", c=4)), reads=["ptf"], writes=["uT"])

    def conv_taps(j0, j1):
        for j in range(j0, j1):
            for c in range(4):
                if j == 0:
                    S.add("dve", R.tensor_scalar(out=yT[:, c, :], in0=uT[:, c, j:j + 128], scalar1=convwT[:, c, j:j + 1], scalar2=None, op0=ALU.mult), reads=["uT", "convwT"], writes=[f"yT{c}"])
                else:
                    S.add("dve", R.scalar_tensor_tensor(out=yT[:, c, :], in0=uT[:, c, j:j + 128], scalar=convwT[:, c, j:j + 1], in1=yT[:, c, :], op0=ALU.mult, op1=ALU.add), reads=["uT", "convwT", f"yT{c}"], writes=[f"yT{c}"])

    def conv_tail(t):
        yk = [f"yT{c}" for c in range(4)]
        S.add("act", R.copy(out=uT[:, :, 0:30], in_=uT[:, :, 128:158]), reads=["uT"] + yk, writes=["uT"])
        for c in range(4):
            S.add("pe", R.transpose(out=ptf[:, c * 128:(c + 1) * 128], in_=yT[:, c, :], identity=identf[:]), reads=[f"yT{c}", "identf"], writes=["ptf"])

    def conv_out(src_ap, src_key, n=128):
        S.add("dve", R.tensor_tensor(out=z_t[0:n], in0=src_ap, in1=convb_bc[0:n], op=ALU.add), reads=[src_key, "convb"], writes=["z_t"])
        S.add("dve", R.reduce_sum(out=st[0:n, 8:9], in_=z_t[0:n], axis=AX.X), reads=["z_t"], writes=["st8"])
        S.add("act", R.activation(out=junk[0:n, 0:512], in_=z_t[0:n], func=AF.Square, accum_out=st[0:n, 9:10]), reads=["z_t"], writes=["junk", "st9"])
        S.add("dve", R.tensor_scalar(out=st[0:n, 8:9], in0=st[0:n, 8:9], scalar1=1.0 / 512, scalar2=None, op0=ALU.mult), reads=["st8"], writes=["st8"])
        S.add("dve", R.tensor_tensor(out=st[0:n, 10:11], in0=st[0:n, 8:9], in1=st[0:n, 8:9], op=ALU.mult), reads=["st8"], writes=["st10"])
        S.add("dve", R.scalar_tensor_tensor(out=st[0:n, 10:11], in0=st[0:n, 10:11], scalar=-512.0, in1=st[0:n, 9:10], op0=ALU.mult, op1=ALU.add), reads=["st10", "st9"], writes=["st10"])
        S.add("act", R.activation(out=st[0:n, 11:12], in_=st[0:n, 10:11], func=AF.Ln, bias=eps_t[0:n, 0:1], scale=1.0 / 512), reads=["st10", "eps"], writes=["st11"])
        S.add("act", R.activation(out=st[0:n, 11:12], in_=st[0:n, 11:12], func=AF.Exp, scale=-0.5), reads=["st11"], writes=["st11"])
        S.add("dve", R.tensor_scalar(out=z_t[0:n], in0=z_t[0:n], scalar1=st[0:n, 8:9], scalar2=st[0:n, 11:12], op0=ALU.subtract, op1=ALU.mult), reads=["z_t", "st8", "st11"], writes=["z_t"])
        S.add("dve", R.tensor_tensor(out=z_t[0:n], in0=z_t[0:n], in1=clnw_bc[0:n], op=ALU.mult), reads=["z_t", "clnw"], writes=["z_t"])
        S.add("dve", R.tensor_tensor(out=z_t[0:n], in0=z_t[0:n], in1=clnb_bc[0:n], op=ALU.add), reads=["z_t", "clnb"], writes=["z_t"])
        S.add("act", R.activation(out=z2[0:n], in_=z_t[0:n], func=AF.Exp, scale=-1.0), reads=["z_t"], writes=["z2"])
        S.add("dve", R.tensor_scalar(out=z2[0:n], in0=z2[0:n], scalar1=1.0, scalar2=None, op0=ALU.add), reads=["z2"], writes=["z2"])
        S.add("dve", R.reciprocal(out=z2[0:n], in_=z2[0:n]), reads=["z2"], writes=["z2"])
        S.add("dve", R.tensor_tensor(out=cat_tm[0:n, 512:1024], in0=z_t[0:n], in1=z2[0:n], op=ALU.mult), reads=["z_t", "z2"], writes=["cat4"])

    def dense_mid(t):
        ck_ = [f"cat{h}" for h in range(5)]
        for k in range(8):
            S.add("pe", R.transpose(out=ptr[:, k, :], in_=cat_tm[:, k * 128:(k + 1) * 128], identity=identb[:]), reads=ck_ + ["identb"], writes=["ptr"])
        S.add("act", R.copy(out=catT[:], in_=ptr[:]), reads=["ptr"], writes=["catT"])
        wk = [f"w_out{k}" for k in range(8)]
        for hf in range(2):
            for k in range(8):
                S.add("pe", R.matmul(pj[hf][:], lhsT=catT[:, k, :], rhs=w_out_b[:, k, hf * 512:(hf + 1) * 512], start=(k == 0), stop=(k == 7)), reads=["catT"] + wk, writes=[f"pj{hf}"])
            S.add("act", R.activation(out=junk[:, 0:512], in_=pj[hf][:], func=AF.Square, accum_out=st[:, 12 + hf:13 + hf]), reads=[f"pj{hf}"], writes=["junk", f"st{12 + hf}"])
        S.add("dve", R.tensor_tensor(out=st[:, 14:15], in0=st[:, 12:13], in1=st[:, 13:14], op=ALU.add), reads=["st12", "st13"], writes=["st14"])
        rstd_from_ss(S, st[:, 14:15], 1024, st[:, 15:16], "st14", "st15")
        for hf in range(2):
            S.add("dve", R.scalar_tensor_tensor(out=d_all[:, t, hf * 512:(hf + 1) * 512], in0=pj[hf][:], scalar=st[:, 15:16], in1=gpost_bc[:, hf * 512:(hf + 1) * 512], op0=ALU.mult, op1=ALU.mult), reads=[f"pj{hf}", "st15", "gpost"], writes=[f"d{t}"])

    parts = [(0, 8), (8, 16), (16, 24), (24, 31)]
    if DBG_TILES > 0:
        dense_in(0)
    for t in range(DBG_TILES):
        if t + 1 < DBG_TILES:
            pre_x(t + 1)
        conv_u(t)
        for h in range(4):
            conv_taps(*parts[h])
            attn_head(t, h)
        if t + 1 < DBG_TILES:
            dense_proj(t + 1)
        conv_tail(t)
        conv_out(ptf[:], "ptf")
        dense_mid(t)

    S.emit()
    st1a.close()
    if STOP_AFTER == '1a':
        st1.close(); outer.close(); _SEM_STACKS.clear(); return nc
    S = Sched(nc)
    st1b = contextlib.ExitStack()
    ptf = ps("ptf_b", [128, 512], F32, st1b)
    kTp = [ps(f"kTp{i}", [128, 8, 128], BF16, st1b) for i in range(2)]
    pss = [ps(f"pss{i}", [128, 512], F32, st1b) for i in range(2)]
    ptf32 = sb("ptf32", [128, 4, 64], F32, st1b); pti = sb("pti", [128, 4, 64], I32, st1b)
    idxf = sb("idxf", [128, 4, 64], F32, st1b); idxi = sb("idxi", [128, 4, 64], I32, st1b)
    NB = 3
    kpg = [sb(f"kpg{i}", [128, 512], BF16, st1b) for i in range(NB)]; vpg = [sb(f"vpg{i}", [128, 512], BF16, st1b) for i in range(NB)]
    kTs = [sb(f"kTs{i}", [128, 4, 128], BF16, st1b) for i in range(2)]
    Vn = sb("Vn", [128, 512], BF16, st1b); qkTT = sb("qkTT", [128, 8, 128], BF16, st1b)
    Qbd = [sb(f"Qbd{i}", [128, 4, 2], BF16, st1b) for i in range(4)]
    Ps = [[sb(f"Ps{i}_{b}", [128, 8, 4], BF16, st1b) for b in range(2)] for i in range(4)]
    posT = sb("posT", [2, 128], BF16, st1b); abias = sb("abias", [2, 65, 8], BF16, st1b)
    mnew = sb("mnew", [128, 4, 8], BF16, st1b); ones4 = sb("ones4", [128, 4], BF16, st1b)
    dmask = sb("dmask", [128, 8, 4], F32, st1b); ssum = sb("ssum", [128, 8, 4], F32, st1b); rsum = sb("rsum", [128, 8], F32, st1b)
    ufull = [sb(f"ufull{i}", [31, 512], F32, st1b) for i in range(2)]; prod = sb("prod", [31, 512], F32, st1b)
    S.add("pool", R.dma_start(out=posT[:], in_=c_posT), writes=["posT"], dma=True)
    S.add("pool", R.dma_start(out=abias[:], in_=c_abias), writes=["abias"], dma=True)
    S.add("pool", R.dma_start(out=mnew[:], in_=c_mnew), writes=["mnew"], dma=True)
    S.add("sp", R.dma_start(out=dmask[:], in_=c_dmask), writes=["dmask"], dma=True)
    S.add("pool", R.memset(ones4[:], 1.0), writes=["ones4"])
    S.add("pool", R.memset(cat_tm[:], 0.0), writes=["cat0", "cat1", "cat2", "cat3", "cat4"])
    for i in range(4):
        S.add("pool", R.memset(Qbd[i][:], 0.0), writes=[f"Qbd{i}"])
        for b in range(2):
            S.add("pool", R.memset(Ps[i][b][:], 0.0), writes=[f"Ps{i}_{b}"])
    dense_in(TS)
    sl = 0
    S.add("sp", R.dma_start(out=pti[:].rearrange("p s j -> p (s j)"), in_=ptab.partition_broadcast(128)), writes=["pti"], dma=True)
    S.add("dve", R.tensor_copy(out=ptf32[:], in_=pti[:]), reads=["pti"], writes=["ptf32"])
    S.add("dve", R.tensor_scalar(out=idxf[:], in0=ptf32[:], scalar1=128.0, scalar2=iot[:, 0:1], op0=ALU.mult, op1=ALU.add), reads=["ptf32", "iot"], writes=["idxf"])
    S.add("dve", R.tensor_copy(out=idxi[:], in_=idxf[:]), reads=["idxf"], writes=["idxi"])
    for s in range(4):
        S.add("dve", R.tensor_copy(out=Qbd[s][0:64, :, 0], in_=qkTT[0:64, 0:4, s]), reads=["qkTT", f"Qbd{s}"], writes=[f"Qbd{s}"])
        S.add("dve", R.tensor_copy(out=Qbd[s][64:128, :, 1], in_=qkTT[64:128, 0:4, s]), reads=["qkTT", f"Qbd{s}"], writes=[f"Qbd{s}"])
    zeros8 = sb("zeros8", [128, 8], BF16, st1b); shiftm = sb("shiftm", [8, 4], F32, st1b); accS = sb("accS", [8, 512], F32, st1b)
    S.add("pool", R.memset(zeros8[:], 0.0), writes=["zeros8"])
    S.add("sp", R.dma_start(out=shiftm[:], in_=c_shift), writes=["shiftm"], dma=True)
    S.add("pe", R.matmul(pj[0][0:8, :], lhsT=zeros8[:], rhs=w_out_b[:, 0, 0:512], start=True, stop=False), reads=["zeros8", "qkTT", "Vn"], writes=["pj0"])
    it = 0
    nsteps = 4 * (NPAGE + 1)
    for s in range(4):
        for jp in range(NPAGE + 1):
            b3 = it % NB; b2 = it % 2
            if jp < NPAGE:
                S.add("pool", R.indirect_dma_start(out=kpg[b3][:], out_offset=None, in_=ck, in_offset=bass.IndirectOffsetOnAxis(ap=idxi[:, s, jp:jp + 1], axis=0)), reads=["idxi"], writes=[f"kpg{b3}"], dma=True)
                S.add("pool", R.indirect_dma_start(out=vpg[b3][:], out_offset=None, in_=cv, in_offset=bass.IndirectOffsetOnAxis(ap=idxi[:, s, jp:jp + 1], axis=0)), reads=["idxi"], writes=[f"vpg{b3}"], dma=True)
                for c in range(4):
                    S.add("pe", R.transpose(out=kTp[b2][:, c, :], in_=kpg[b3][:, c * 128:(c + 1) * 128], identity=identb[:]), reads=[f"kpg{b3}"], writes=[f"kTp{b2}"])
                S.add("act", R.copy(out=kTs[b2][:], in_=kTp[b2][:, 0:4, :]), reads=[f"kTp{b2}"], writes=[f"kTs{b2}"])
                kt_ = lambda c: kTs[b2][:, c, :]
                kkeys = [f"kTs{b2}"]
                vt_ = vpg[b3]; vkeys = [f"vpg{b3}"]
            else:
                kt_ = lambda c: qkTT[:, 4 + c, :]
                kkeys = ["qkTT"]
                vt_ = Vn; vkeys = ["Vn"]
            if jp < NPAGE:
                S.add("pe", R.matmul(pss[b2][:, 0:8], lhsT=posT[:, :], rhs=abias[:, jp, :], start=True, stop=False), reads=["posT", "abias"], writes=[f"pss{b2}"])
            else:
                S.add("pe", R.matmul(pss[b2][:, 0:8], lhsT=identb[:], rhs=mnew[:, s, :], start=True, stop=False), reads=["mnew"], writes=[f"pss{b2}"])
            for c in range(4):
                S.add("pe", R.matmul(pss[b2][:, 2 * c:2 * c + 2], lhsT=kt_(c), rhs=Qbd[s][:, c, :], start=False, stop=(c == 3)), reads=kkeys + [f"Qbd{s}"], writes=[f"pss{b2}"])
            S.add("act", R.activation(out=Ps[s][b2][:, :, s], in_=pss[b2][:, 0:8], func=AF.Exp), reads=[f"pss{b2}"], writes=[f"Ps{s}_{b2}"])
            first = (it == 0); last = (it == nsteps - 1)
            for h in range(4):
                S.add("pe", R.matmul(pj[0][0:8, h * 128:(h + 1) * 128], lhsT=Ps[s][b2][:, 2 * h:2 * h + 2, :].rearrange("p a b -> p (a b)"), rhs=vt_[:, h * 128:(h + 1) * 128], start=False, stop=(last and h == 3)), reads=[f"Ps{s}_{b2}"] + vkeys, writes=["pj0"])
            S.add("pe", R.matmul(ptf[0:4, 0:32], lhsT=ones4[:], rhs=Ps[s][b2][:].rearrange("p a b -> p (a b)"), start=first, stop=last), reads=[f"Ps{s}_{b2}", "ones4"], writes=["ptf"])
            it += 1
    S.add("dve", R.tensor_tensor(out=ssum[0:4], in0=ptf[0:4, 0:32].rearrange("p (a b) -> p a b", a=8), in1=dmask[0:4], op=ALU.mult), reads=["ptf", "dmask"], writes=["ssum"])
    S.add("dve", R.reduce_sum(out=rsum[0:4], in_=ssum[0:4], axis=AX.X), reads=["ssum"], writes=["rsum"])
    S.add("dve", R.reciprocal(out=rsum[0:4], in_=rsum[0:4]), reads=["rsum"], writes=["rsum"])
    S.add("act", R.copy(out=accS[:], in_=pj[0][0:8, :]), reads=["pj0"], writes=["accS"])
    S.add("pe", R.matmul(pj[1][0:4, :], lhsT=shiftm[:], rhs=accS[:], start=True, stop=True), reads=["accS", "shiftm"], writes=["pj1"])
    for h in range(4):
        post_attn(h, pj[0][0:4, h * 128:(h + 1) * 128], pj[1][0:4, h * 128:(h + 1) * 128], rsum[0:4, 2 * h:2 * h + 1], rsum[0:4, 2 * h + 1:2 * h + 2], ["pj0", "pj1", "rsum"], n=4)
    for s in range(4):
        ub = ufull[s % 2]
        S.add("sp", R.dma_start(out=ub[0:30, :], in_=state[s]), writes=[f"ufull{s % 2}"], dma=True)
        S.add("sp", R.dma_start(out=ub[30:31, :], in_=u_tm[sl][s:s + 1, :]), reads=[f"u_tm{sl}"], writes=[f"ufull{s % 2}b"], dma=True)
        S.add("sp", R.dma_start(out=nc_s[s], in_=ub[1:31, :]), reads=[f"ufull{s % 2}", f"ufull{s % 2}b"], dma=True)
        S.add("dve", R.tensor_tensor(out=prod[:], in0=ub[:], in1=convw_tm[:], op=ALU.mult), reads=[f"ufull{s % 2}", f"ufull{s % 2}b", "convw_tm"], writes=["prod"])
        S.add("pe", R.matmul(ptf[0:4, :], lhsT=sel[:, s, :], rhs=prod[:], start=(s == 0), stop=(s == 3)), reads=["prod", "sel"], writes=["ptf"])
    conv_out(ptf[0:4, :], "ptf", n=4)
    dense_mid(TS)
    S.emit()
    st1b.close()
    st1.close()
    if STOP_AFTER == '1b':
        outer.close(); _SEM_STACKS.clear(); return nc

    st2 = contextlib.ExitStack()
    S = Sched(nc)
    wg = sb("wg", [128, 8, 2816], BF16, st2); wu = sb("wu", [128, 8, 2816], BF16, st2); wd = sb("wd", [128, 22, 1024], BF16, st2)
    gfpre_bc = sb("gfpre_bc", [128, 1024], F32, st2); gfpost_bc = sb("gfpost_bc", [128, 1024], F32, st2)
    xs2 = [sb(f"xm{i}", [128, 1024], F32, st2) for i in range(2)]
    junk2 = sb("junk2", [128, 1024], BF16, st2); xn2 = sb("xn2", [128, 1024], BF16, st2); hTs = [sb(f"hT{i}", [128, 8, 128], BF16, st2) for i in range(2)]
    actT = sb("actT", [128, 22, 128], BF16, st2); gsl = [sb(f"g2_{i}", [128, 512], F32, st2) for i in range(2)]
    st_ = sb("stat2", [128, 8], F32, st2); ob = [sb("ob0", [128, 1024], F32, st2)] * 2
    pf = [ps(f"pf{i}", [128, 512], F32, st2) for i in range(2)]
    pk = [ps(f"pk{i}", [128, 512], F32, st2) for i in range(2)]
    for k in range(8):
        S.add("pool", R.dma_start(out=wg[:, k, :], in_=w_gate[k * 128:(k + 1) * 128, :]), writes=[f"wg{k}"], dma=True)
        S.add("pool", R.dma_start(out=wu[:, k, :], in_=w_up[k * 128:(k + 1) * 128, :]), writes=[f"wu{k}"], dma=True)
    for f in range(22):
        S.add("pool", R.dma_start(out=wd[:, f, :], in_=w_down[f * 128:(f + 1) * 128, :]), writes=[f"wd{f}"], dma=True)
    S.add("sp", R.dma_start(out=gfpre_bc[:], in_=g_fpre.partition_broadcast(128)), writes=["gfpre"], dma=True)
    S.add("sp", R.dma_start(out=gfpost_bc[:], in_=g_fpost.partition_broadcast(128)), writes=["gfpost"], dma=True)
    wgk = [f"wg{k}" for k in range(8)]; wuk = [f"wu{k}" for k in range(8)]; wdk = [f"wd{f}" for f in range(22)]
    tiles = list(DBG_FFN_TILES if DBG_FFN_TILES is not None else range(NTT))

    def pre(t):
        sl = t % 2
        xm = xs2[sl]; h_t = hTs[sl]
        S.add("sp", R.dma_start(out=xm[:], in_=xsrc(t)), writes=[f"xm{sl}"], dma=True)
        S.add("dve", R.tensor_tensor(out=xm[:], in0=xm[:], in1=d_all[:, t, :], op=ALU.add), reads=[f"xm{sl}"], writes=[f"xm{sl}"])
        S.add("act", R.activation(out=junk2[:], in_=xm[:], func=AF.Square, accum_out=st_[:, 0:1]), reads=[f"xm{sl}"], writes=["junk2", "s0"])
        rstd_from_ss(S, st_[:, 0:1], 1024, st_[:, 1:2], "s0", "s1")
        S.add("dve", R.scalar_tensor_tensor(out=xn2[:], in0=xm[:], scalar=st_[:, 1:2], in1=gfpre_bc[:], op0=ALU.mult, op1=ALU.mult), reads=[f"xm{sl}", "s1", "gfpre"], writes=["xn2"])
        for k in range(8):
            S.add("pe", R.transpose(out=ptr[:, k, :], in_=xn2[:, k * 128:(k + 1) * 128], identity=identb[:]), reads=["xn2"], writes=["ptr"])
        S.add("act", R.copy(out=h_t[:], in_=ptr[:]), reads=["ptr"], writes=[f"hT{sl}"])

    def gateup(t):
        sl = t % 2
        h_t = hTs[sl]
        for gi, f0 in enumerate(range(0, 22, 4)):
            fs = list(range(f0, min(f0 + 4, 22)))
            n = len(fs) * 128
            pg, pu = (pj[0], pj[1]) if gi % 2 == 0 else (pk[0], pk[1])
            kg, ku = ("pj0", "pj1") if gi % 2 == 0 else ("pk0", "pk1")
            gs = gsl[gi % 2]
            for i, f in enumerate(fs):
                for k in range(8):
                    S.add("pe", R.matmul(pg[:, i * 128:(i + 1) * 128], lhsT=wg[:, k, f * 128:(f + 1) * 128], rhs=h_t[:, k, :], start=(k == 0), stop=(k == 7)), reads=[f"hT{sl}"] + wgk, writes=[kg])
            for i, f in enumerate(fs):
                for k in range(8):
                    S.add("pe", R.matmul(pu[:, i * 128:(i + 1) * 128], lhsT=wu[:, k, f * 128:(f + 1) * 128], rhs=h_t[:, k, :], start=(k == 0), stop=(k == 7)), reads=[f"hT{sl}"] + wuk, writes=[ku])
            S.add("act", R.activation(out=gs[:, 0:n], in_=pg[:, 0:n], func=AF.Silu), reads=[kg], writes=[f"g2_{gi % 2}"])
            S.add("dve", R.tensor_tensor(out=actT[:, f0:f0 + n // 128, :].rearrange("p f t -> p (f t)"), in0=pu[:, 0:n], in1=gs[:, 0:n], op=ALU.mult), reads=[ku, f"g2_{gi % 2}"], writes=[f"actT{f0}"])

    def down_post(t):
        sl = t % 2
        xm = xs2[sl]
        ak = [f"actT{f0}" for f0 in range(0, 22, 4)]
        for hf in range(2):
            for f in range(22):
                S.add("pe", R.matmul(pf[hf][:], lhsT=actT[:, f, :], rhs=wd[:, f, hf * 512:(hf + 1) * 512], start=(f == 0), stop=(f == 21)), reads=ak + wdk, writes=[f"pf{hf}"])
            S.add("act", R.activation(out=junk2[:, 0:512], in_=pf[hf][:], func=AF.Square, accum_out=st_[:, 2 + hf:3 + hf]), reads=[f"pf{hf}"], writes=["junk2", f"s{2 + hf}"])
        S.add("dve", R.tensor_tensor(out=st_[:, 4:5], in0=st_[:, 2:3], in1=st_[:, 3:4], op=ALU.add), reads=["s2", "s3"], writes=["s4"])
        rstd_from_ss(S, st_[:, 4:5], 1024, st_[:, 5:6], "s4", "s5")
        o_t = ob[0]
        for hf in range(2):
            S.add("dve", R.scalar_tensor_tensor(out=o_t[:, hf * 512:(hf + 1) * 512], in0=pf[hf][:], scalar=st_[:, 5:6], in1=gfpost_bc[:, hf * 512:(hf + 1) * 512], op0=ALU.mult, op1=ALU.mult), reads=[f"pf{hf}", "s5", "gfpost"], writes=["ob0"])
        S.add("dve", R.tensor_tensor(out=o_t[:], in0=o_t[:], in1=xm[:], op=ALU.add), reads=["ob0", f"xm{sl}"], writes=["ob0"])
        if t == 0:
            S.add("sp", R.dma_start(out=y_p[0:112, :], in_=o_t[16:128, :]), reads=["ob0"], dma=True)
        elif t < 16:
            S.add("sp", R.dma_start(out=y_p[t * 128 - 16:t * 128 + 112, :], in_=o_t[:, :]), reads=["ob0"], dma=True)
        elif t == 16:
            S.add("sp", R.dma_start(out=y_p[2032:2048, :], in_=o_t[0:16, :]), reads=["ob0"], dma=True)
        else:
            S.add("sp", R.dma_start(out=y_s, in_=o_t[0:4, :]), reads=["ob0"], dma=True)

    pre(tiles[0])
    for i, t in enumerate(tiles):
        gateup(t)
        if i + 1 < len(tiles):
            pre(tiles[i + 1])
        down_post(t)
    S.emit()
    st2.close()
    for s_ in _SEM_STACKS:
        s_.close()
    _SEM_STACKS.clear()
    outer.close()
    return nc


def _consts():
    c = {}
    c["c_identf"] = np.eye(128, dtype=np.float32)
    k = np.arange(128)[:, None]; q = np.arange(128)[None, :]
    c["c_mask"] = np.where(k > q, -30000.0, 0.0).astype(np.float32)
    npos = NT * 128
    pos = np.arange(npos); a = pos // 128; b = pos % 128
    qx = np.zeros((4, 8, npos), np.float32); kx = np.zeros((4, 8, npos), np.float32)
    for pr in range(8):
        s = SL[pr // 2]
        qx[0, pr] = -s * 128 * a; qx[1, pr] = -s * b; qx[2, pr] = 1; qx[3, pr] = 1
        kx[0, pr] = 1; kx[1, pr] = 1; kx[2, pr] = s * 128 * a; kx[3, pr] = s * b
    c["c_qx"] = qx; c["c_kx"] = kx
    posT = np.zeros((2, 128), np.float32); posT[0] = 1.0; posT[1] = np.arange(128)
    abias = np.zeros((2, 65, 8), np.float32)
    for pr in range(8):
        s = SL[pr // 2]
        abias[0, :64, pr] = -s * (8192 - 128 * np.arange(64)); abias[1, :64, pr] = s
    mnew = np.full((128, 4, 8), -30000.0, np.float32)
    dmask = np.zeros((128, 8, 4), np.float32)
    for s in range(4):
        mnew[s, s, :] = 0.0
        dmask[s, :, s] = 1.0
    shiftm = np.zeros((8, 4), np.float32)
    for s in range(4):
        shiftm[4 + s, s] = 1.0
    c["c_shift"] = shiftm
    c["c_posT"] = posT; c["c_abias"] = abias; c["c_mnew"] = mnew; c["c_dmask"] = dmask
    sel = np.zeros((31, 4, 4), np.float32)
    for s in range(4):
        sel[:, s, s] = 1.0
    c["c_sel"] = sel
    c["c_iota"] = np.arange(128, dtype=np.float32).reshape(128, 1)
    return c


def kernel(x_prompt, x_sample, cache_k, cache_v, state_conv, page_table, meta_tokens,
           ln_mix_pre, ln_mix_post, w_in, lambda_q1, lambda_k1, lambda_q2, lambda_k2,
           subln_w, conv_w, conv_b, conv_ln_w, conv_ln_b, w_out, ln_ffn_pre, ln_ffn_post,
           w_gate, w_up, w_down):
    f = lambda a: np.ascontiguousarray(np.asarray(a, dtype=np.float32))
    nc = build()
    consts = _consts()
    ck = f(cache_k).reshape(2560 * 128, 512)[:CK_ROWS]; cv = f(cache_v).reshape(2560 * 128, 512)[:CK_ROWS]
    shared = {
        "ck": ck, "cv": cv, "w_in": f(w_in)[0], "w_out": f(w_out)[0], "w_gate": f(w_gate)[0], "w_up": f(w_up)[0], "w_down": f(w_down)[0],
        "g_pre": f(ln_mix_pre), "g_post": f(ln_mix_post), "g_fpre": f(ln_ffn_pre), "g_fpost": f(ln_ffn_post),
        "subln": f(subln_w), "conv_w": f(conv_w)[0], "conv_b": f(conv_b), "cln_w": f(conv_ln_w), "cln_b": f(conv_ln_b),
        "lq1": f(lambda_q1), "lk1": f(lambda_k1), "lq2": f(lambda_q2), "lk2": f(lambda_k2),
    }
    shared.update(consts)
    xpn = f(x_prompt); xsn = f(x_sample); meta = f(meta_tokens); stc = f(state_conv)[0]
    pt = np.ascontiguousarray(np.asarray(page_table, dtype=np.int32))
    in_maps = []
    for c in range(NCORES):
        xp = np.zeros((NT * 128, 1024), np.float32)
        xp[0:16] = meta; xp[16:2064] = xpn[c]
        xs4 = np.zeros((128, 1024), np.float32); xs4[0:4] = xsn[4 * c:4 * c + 4, 0]
        m = dict(shared)
        m.update({"xp": xp, "xs4": xs4, "ptab": np.ascontiguousarray(pt[4 * c:4 * c + 4]).reshape(1, 256), "state": np.ascontiguousarray(stc[4 * c:4 * c + 4])})
        in_maps.append(m)
    res = run_bass_kernel_spmd(nc, in_maps, core_ids=list(range(NCORES)))
    R = res.results
    y_p = np.stack([R[c]["y_p"] for c in range(NCORES)])
    y_s = np.concatenate([R[c]["y_s"] for c in range(NCORES)])[:, None, :]
    nk_p = np.stack([R[c]["nk_p"] for c in range(NCORES)]).reshape(1, NCORES, 2064, 8, 64)
    nv_p = np.stack([R[c]["nv_p"] for c in range(NCORES)]).reshape(1, NCORES, 2064, 4, 128)
    nc_p = np.stack([R[c]["nc_p"] for c in range(NCORES)]).reshape(1, NCORES, 30, 512)
    nk_s = np.concatenate([R[c]["nk_s"] for c in range(NCORES)]).reshape(1, 4 * NCORES, 1, 8, 64)
    nv_s = np.concatenate([R[c]["nv_s"] for c in range(NCORES)]).reshape(1, 4 * NCORES, 1, 4, 128)
    nc_s = np.concatenate([R[c]["nc_s"] for c in range(NCORES)]).reshape(1, 4 * NCORES, 30, 512)
    return (y_p.astype(np.float32), y_s.astype(np.float32), nk_p, nv_p, nc_p, nk_s, nv_s, nc_s)
```

```python
GEN = 4000
DBG_SKIP = set()
DBG_SKIP_METH = set()
_SEM_STACKS = []
NDMA = 12


class _Rec:
    def __getattr__(self, name):
        def f(*a, **k):
            return (name, a, k)
        return f
R = _Rec()

class Op:
    __slots__ = ("eng", "fn", "deps", "dma", "idx", "need_inc", "cnt", "dsem", "dval", "prev_dma")

    def __init__(self, eng, fn, dma):
        self.eng = eng
        self.fn = fn
        self.dma = dma
        self.deps = set()
        self.need_inc = False
        self.cnt = None
        self.dsem = None
        self.dval = None
        self.prev_dma = None


class Sched:
    ENGS = ("pe", "act", "dve", "pool", "sp")

    def __init__(self, nc, same_engine_sync=True):
        self.nc = nc
        self.ops = []
        self.last_w = {}
        self.readers = {}
        self.same = same_engine_sync

    def add(self, eng, fn, reads=(), writes=(), dma=False):
        if any(k in DBG_SKIP for k in writes) or (DBG_SKIP_METH and fn[0] in DBG_SKIP_METH):
            return None
        op = Op(eng, fn, dma)
        op.idx = len(self.ops)
        for k in reads:
            w = self.last_w.get(k)
            if w is not None:
                op.deps.add(w)
            if k[:2] in ("pj", "pt", "ps", "pO", "pf"):
                for r in self.readers.get(k, ()):
                    op.deps.add(r)
        for k in writes:
            w = self.last_w.get(k)
            if w is not None:
                op.deps.add(w)
            for r in self.readers.get(k, ()):
                op.deps.add(r)
        for k in reads:
            self.readers.setdefault(k, []).append(op.idx)
        for k in writes:
            self.last_w[k] = op.idx
            self.readers[k] = []
        op.deps.discard(op.idx)
        self.ops.append(op)
        return op

    def emit(self, final_wait=True):
        nc = self.nc
        ops = self.ops
        for op in ops:
            for d in op.deps:
                dop = ops[d]
                if dop.dma:
                    continue
                if dop.eng == op.eng and not op.dma:
                    if dop.eng == "pe" or not self.same:
                        continue
                dop.need_inc = True
        cnt = {e: 0 for e in self.ENGS}
        dcount = {e: 0 for e in self.ENGS}
        for op in ops:
            if op.dma:
                op.cnt = dcount[op.eng]
                dcount[op.eng] += 1
            elif op.need_inc:
                cnt[op.eng] += 1
                op.cnt = cnt[op.eng]
        import contextlib
        stack = contextlib.ExitStack()
        sems = {}
        for e in self.ENGS:
            ngen = cnt[e] // GEN + 1
            sems[e] = [stack.enter_context(nc.semaphore(f"s_{e}_{g}_{id(self)%9973}")) for g in range(ngen)]
        dsems = {}
        for e in self.ENGS:
            n = min(NDMA, dcount[e])
            dsems[e] = [stack.enter_context(nc.semaphore(f"d_{e}_{i}_{id(self)%9973}")) for i in range(n)]
        slot_last = {}
        slot_val = {}
        for op in ops:
            if op.dma:
                s = op.cnt % NDMA
                key = (op.eng, s)
                op.dsem = dsems[op.eng][s]
                slot_val[key] = slot_val.get(key, 0) + 16
                op.dval = slot_val[key]
                op.prev_dma = slot_last.get(key)
                slot_last[key] = op
        per_eng = {e: [] for e in self.ENGS}
        for op in ops:
            per_eng[op.eng].append(op)
        last_dma_per_eng = {e: [o for o in per_eng[e] if o.dma] for e in self.ENGS}

        def run_engine(e, eng):
            seen = {}
            seend = {}

            def wait_c(dop):
                g, v = divmod(dop.cnt - 1, GEN)
                v += 1
                if seen.get((dop.eng, g), 0) >= v:
                    return
                eng.wait_ge(sems[dop.eng][g], v)
                seen[(dop.eng, g)] = v
                for gg in range(g):
                    seen[(dop.eng, gg)] = GEN

            def wait_d(dop):
                k = id(dop.dsem)
                if seend.get(k, 0) >= dop.dval:
                    return
                eng.wait_ge(dop.dsem, dop.dval)
                seend[k] = dop.dval

            for op in per_eng[e]:
                for d in sorted(op.deps):
                    dop = ops[d]
                    if dop.dma:
                        wait_d(dop)
                    else:
                        if dop.eng == e and not op.dma and (e == "pe" or not self.same):
                            continue
                        if dop.cnt is None:
                            continue
                        wait_c(dop)
                if op.dma and op.prev_dma is not None:
                    wait_d(op.prev_dma)
                ins = getattr(eng, op.fn[0])(*op.fn[1], **op.fn[2])
                if op.dma:
                    ins.then_inc(op.dsem, 16)
                elif op.need_inc:
                    g = (op.cnt - 1) // GEN
                    ins.then_inc(sems[e][g], 1)
            if final_wait:
                lastd = {}
                for o in last_dma_per_eng[e]:
                    lastd[id(o.dsem)] = o
                for o in lastd.values():
                    wait_d(o)

        with nc.Block() as block:
            if per_eng["sp"]:
                @block.sync
                def _(eng):
                    run_engine("sp", eng)
            if per_eng["act"]:
                @block.scalar
                def _(eng):
                    run_engine("act", eng)
            if per_eng["dve"]:
                @block.vector
                def _(eng):
                    run_engine("dve", eng)
            if per_eng["pool"]:
                @block.gpsimd
                def _(eng):
                    run_engine("pool", eng)
            if per_eng["pe"]:
                @block.tensor
                def _(eng):
                    run_engine("pe", eng)
        _SEM_STACKS.append(stack)


import numpy as np
import concourse.bass as bass
import concourse.mybir as mybir
from concourse.bass_utils import run_bass_kernel_spmd

F32 = mybir.dt.float32
BF16 = mybir.dt.bfloat16
I32 = mybir.dt.int32
AF = mybir.ActivationFunctionType
ALU = mybir.AluOpType
AX = mybir.AxisListType

NT = 17
TS = 17
NTT = 18
EPS = 1e-6
NPAGE = 64
SL = [2.0 ** (-2 * (h + 1)) for h in range(4)]


STOP_AFTER = 'all'
SKIP_SAMPLE_ATTN = False
NCORES = 8
CK_ROWS = 2560 * 128
DBG_TILES = NT
DBG_STAGES = 4
DBG_FFN_TILES = None


def build():
    nc = bass.Bass("TRN2", target_bir_lowering=False)
    def din(name, shape, dt=F32):
        return nc.dram_tensor(name, list(shape), dt, kind="ExternalInput").ap()
    def dout(name, shape):
        return nc.dram_tensor(name, list(shape), F32, kind="ExternalOutput").ap()
    xp = din("xp", [NT * 128, 1024]); xs4 = din("xs4", [128, 1024])
    ck = din("ck", [CK_ROWS, 512]); cv = din("cv", [CK_ROWS, 512])
    ptab = din("ptab", [1, 256], I32); state = din("state", [4, 30, 512])
    w_in = din("w_in", [1024, 2560]); w_out = din("w_out", [1024, 1024])
    w_gate = din("w_gate", [1024, 2816]); w_up = din("w_up", [1024, 2816]); w_down = din("w_down", [2816, 1024])
    g_pre = din("g_pre", [1, 1024]); g_post = din("g_post", [1, 1024]); g_fpre = din("g_fpre", [1, 1024]); g_fpost = din("g_fpost", [1, 1024])
    subln = din("subln", [1, 128]); conv_w = din("conv_w", [31, 512]); conv_b = din("conv_b", [1, 512])
    cln_w = din("cln_w", [1, 512]); cln_b = din("cln_b", [1, 512])
    lq1 = din("lq1", [1, 64]); lk1 = din("lk1", [1, 64]); lq2 = din("lq2", [1, 64]); lk2 = din("lk2", [1, 64])
    c_identf = din("c_identf", [128, 128]); c_mask = din("c_mask", [128, 128])
    c_qx = din("c_qx", [4, 8, NT * 128]); c_kx = din("c_kx", [4, 8, NT * 128])
    c_posT = din("c_posT", [2, 128]); c_abias = din("c_abias", [2, 65, 8]); c_mnew = din("c_mnew", [128, 4, 8]); c_dmask = din("c_dmask", [128, 8, 4]); c_shift = din("c_shift", [8, 4])
    c_sel = din("c_sel", [31, 4, 4]); c_iota = din("c_iota", [128, 1])
    y_p = dout("y_p", [2048, 1024]); y_s = dout("y_s", [4, 1024])
    nk_p = dout("nk_p", [2064, 512]); nv_p = dout("nv_p", [2064, 512]); nc_p = dout("nc_p", [30, 512])
    nk_s = dout("nk_s", [4, 512]); nv_s = dout("nv_s", [4, 512]); nc_s = dout("nc_s", [4, 30, 512])

    import contextlib
    outer = contextlib.ExitStack()
    def sb(name, shape, dt=F32, st=None):
        return (st or outer).enter_context(nc.sbuf_tensor(name, list(shape), dt))
    def ps(name, shape, dt=F32, st=None):
        return (st or outer).enter_context(nc.psum_tensor(name, list(shape), dt))
    d_all = sb("d_all", [128, NTT, 1024], BF16)
    identf = sb("identf", [128, 128]); identb = sb("identb", [128, 128], BF16)
    eps_t = sb("eps_t", [128, 1])
    ptr = ps("ptr", [128, 8, 128], BF16)
    pj = [ps(f"pj{i}", [128, 512]) for i in range(2)]

    def xsrc(t):
        return xs4 if t == TS else xp[t * 128:(t + 1) * 128, :]

    def rstd_from_ss(S, ss_ap, n, out_ap, key_in, key_out):
        S.add("act", R.activation(out=out_ap, in_=ss_ap, func=AF.Ln, bias=eps_t[:, 0:1], scale=1.0 / n), reads=[key_in, "eps"], writes=[key_out])
        S.add("act", R.activation(out=out_ap, in_=out_ap, func=AF.Exp, scale=-0.5), reads=[key_out], writes=[key_out])

    st1 = contextlib.ExitStack()
    st1a = contextlib.ExitStack()
    S = Sched(nc)
    w_in_b = sb("w_in_b", [128, 8, 2560], BF16, st1); w_out_b = sb("w_out_b", [128, 8, 1024], BF16, st1)
    qT = sb("qT", [68, 8, 128], BF16, st1);
    gpre_bc = sb("gpre_bc", [128, 1024], F32, st1); gpost_bc = sb("gpost_bc", [128, 1024], F32, st1)
    subln_bc = sb("subln_bc", [128, 128], F32, st1); clnw_bc = sb("clnw_bc", [128, 512], F32, st1); clnb_bc = sb("clnb_bc", [128, 512], F32, st1)
    convb_bc = sb("convb_bc", [128, 512], F32, st1)
    convw_tm = sb("convw_tm", [31, 512], F32, st1); convwT = sb("convwT", [128, 4, 31], F32, st1)
    lamt = sb("lamt", [128, 8, 64], F32, st1); lam = sb("lam", [128, 4], F32, st1)
    maskb = sb("maskb", [128, 128], BF16, st1)
    xs = [sb("xs0", [128, 1024], F32, st1)] * 2
    junk = sb("junk", [128, 1024], BF16, st1)
    xn = sb("xn", [128, 1024], BF16, st1); xnT = sb("xnT", [128, 8, 128], BF16, st1)
    st = sb("stat", [128, 16], F32, st1)
    qk_tm = sb("qk_tm", [128, 1024], BF16, st1); kv_st = [sb("kv_st0", [128, 1024], F32, st1)] * 2
    u_tm = [sb("u_tm0", [128, 512], F32, st1)] * 2
    cat_tm = sb("cat_tm", [128, 1024], BF16, st1); catT = sb("catT", [128, 8, 128], BF16, st1)
    PT = [sb(f"PT{i}", [128, 512], BF16, st1) for i in range(2)]
    o_h = sb("o_h", [128, 128], F32, st1); t1 = sb("t1", [128, 128], F32, st1)
    z_t = sb("z_t", [128, 512], F32, st1); z2 = sb("z2", [128, 512], F32, st1); e_t = z2
    iot = sb("iot", [128, 1], F32, st1); sel = sb("sel", [31, 4, 4], F32, st1)
    ptf = ps("ptf", [128, 512], F32, st1a)
    pst = [ps(f"pst{i}", [128, 512], F32, st1a) for i in range(2)]
    pO_ = [ps(f"pO{i}", [128, 512], F32, st1a) for i in range(2)]
    pO = [p_[:, 0:258].rearrange("p (j e) -> p j e", j=2) for p_ in pO_]
    kT = sb("kT", [68, 8, NT * 128], BF16, st1a); Vaug = sb("Vaug", [128, NT, 4, 129], BF16, st1a)
    uT = sb("uT", [128, 4, 158], F32, st1a); yT = sb("yT", [128, 4, 128], F32, st1a)

    S.add("sp", R.dma_start(out=identf[:], in_=c_identf), writes=["identf"], dma=True)
    S.add("dve", R.tensor_copy(out=identb[:], in_=identf[:]), reads=["identf"], writes=["identb"])
    S.add("dve", R.memset(eps_t[:], EPS), writes=["eps"])
    S.add("pool", R.dma_start(out=maskb[:], in_=c_mask), writes=["maskb"], dma=True)
    S.add("pool", R.dma_start(out=kT[64:68, :, :], in_=c_kx), writes=["kTx"], dma=True)
    for k in range(8):
        S.add("pool", R.dma_start(out=w_in_b[:, k, :], in_=w_in[k * 128:(k + 1) * 128, :]), writes=[f"w_in{k}"], dma=True)
    for k in range(8):
        S.add("pool", R.dma_start(out=w_out_b[:, k, :], in_=w_out[k * 128:(k + 1) * 128, :]), writes=[f"w_out{k}"], dma=True)
    for (t_, src, key) in ((gpre_bc, g_pre, "gpre"), (gpost_bc, g_post, "gpost"), (subln_bc, subln, "subln"), (clnw_bc, cln_w, "clnw"), (clnb_bc, cln_b, "clnb"), (convb_bc, conv_b, "convb")):
        S.add("sp", R.dma_start(out=t_[:], in_=src.partition_broadcast(128)), writes=[key], dma=True)
    S.add("dve", R.tensor_scalar(out=subln_bc[:], in0=subln_bc[:], scalar1=0.8, scalar2=None, op0=ALU.mult), reads=["subln"], writes=["subln"])
    S.add("sp", R.dma_start(out=convw_tm[:], in_=conv_w), writes=["convw_tm"], dma=True)
    S.add("sp", R.dma_start(out=sel[:], in_=c_sel), writes=["sel"], dma=True)
    S.add("sp", R.dma_start(out=iot[:], in_=c_iota), writes=["iot"], dma=True)
    for c in range(4):
        S.add("pe", R.transpose(out=ptf[:, c * 31:(c + 1) * 31], in_=convw_tm[:, c * 128:(c + 1) * 128], identity=identf[0:31, 0:31]), reads=["convw_tm", "identf"], writes=["ptf"])
    S.add("dve", R.tensor_copy(out=convwT[:].rearrange("p c j -> p (c j)"), in_=ptf[:, 0:124]), reads=["ptf"], writes=["convwT"])
    for i, v in enumerate((lq1, lk1, lq2, lk2)):
        S.add("sp", R.dma_start(out=lamt[:, i, :], in_=v.partition_broadcast(128)), writes=[f"lamt{i}"], dma=True)
    S.add("dve", R.tensor_tensor(out=lamt[:, 4, :], in0=lamt[:, 0, :], in1=lamt[:, 1, :], op=ALU.mult), reads=["lamt0", "lamt1"], writes=["lamt4"])
    S.add("dve", R.tensor_tensor(out=lamt[:, 5, :], in0=lamt[:, 2, :], in1=lamt[:, 3, :], op=ALU.mult), reads=["lamt2", "lamt3"], writes=["lamt5"])
    S.add("dve", R.reduce_sum(out=lam[:, 0:2], in_=lamt[:, 4:6, :], axis=AX.X), reads=["lamt4", "lamt5"], writes=["lam"])
    S.add("act", R.activation(out=lam[:, 0:2], in_=lam[:, 0:2], func=AF.Exp), reads=["lam"], writes=["lam"])
    S.add("dve", R.tensor_tensor(out=lam[:, 2:3], in0=lam[:, 1:2], in1=lam[:, 0:1], op=ALU.subtract), reads=["lam"], writes=["lam"])
    S.add("dve", R.tensor_scalar(out=lam[:, 3:4], in0=lam[:, 2:3], scalar1=-0.2, scalar2=None, op0=ALU.add), reads=["lam"], writes=["lam"])
    S.add("pool", R.memset(Vaug[:, :, :, 128:129], 1.0), writes=["Vones"])
    S.add("pool", R.memset(uT[:, :, 0:30], 0.0), writes=["uT"])

    def dense_in(t):
        pre_x(t)
        dense_proj(t)

    def pre_x(t):
        sl = 0
        x_t = xs[sl]
        S.add("sp", R.dma_start(out=x_t[:], in_=xsrc(t)), writes=[f"xs{sl}"], dma=True)
        S.add("act", R.activation(out=junk[:], in_=x_t[:], func=AF.Square, accum_out=st[:, 0:1]), reads=[f"xs{sl}"], writes=["junk", "st0"])
        rstd_from_ss(S, st[:, 0:1], 1024, st[:, 1:2], "st0", "st1")
        S.add("dve", R.scalar_tensor_tensor(out=xn[:], in0=x_t[:], scalar=st[:, 1:2], in1=gpre_bc[:], op0=ALU.mult, op1=ALU.mult), reads=[f"xs{sl}", "st1", "gpre"], writes=["xn"])
        for k in range(8):
            S.add("pe", R.transpose(out=ptr[:, k, :], in_=xn[:, k * 128:(k + 1) * 128], identity=identb[:]), reads=["xn", "identb"], writes=["ptr"])
        S.add("act", R.copy(out=xnT[:], in_=ptr[:]), reads=["ptr"], writes=["xnT"])

    def dense_proj(t):
        sl = 0
        wk = [f"w_in{k}" for k in range(8)]
        def proj(blk, pb):
            for k in range(8):
                S.add("pe", R.matmul(pj[pb][:], lhsT=xnT[:, k, :], rhs=w_in_b[:, k, blk * 512:(blk + 1) * 512], start=(k == 0), stop=(k == 7)), reads=["xnT"] + wk, writes=[f"pj{pb}", f"projblk{blk}"])
        kvs = kv_st[sl]
        proj(0, 0)
        S.add("act", R.mul(out=qk_tm[:, 0:512], in_=pj[0][:], mul=0.125), reads=["pj0"], writes=["qk_q"])
        proj(1, 1)
        S.add("act", R.copy(out=kvs[:, 0:512], in_=pj[1][:]), reads=["pj1"], writes=[f"kvs{sl}k"])
        S.add("act", R.copy(out=qk_tm[:, 512:1024], in_=pj[1][:]), reads=["pj1"], writes=["qk_k"])
        proj(2, 0)
        S.add("act", R.copy(out=kvs[:, 512:1024], in_=pj[0][:]), reads=["pj0"], writes=[f"kvs{sl}v"])
        if t == TS:
            S.add("act", R.copy(out=Vn[:], in_=pj[0][:]), reads=["pj0"], writes=["Vn"])
        else:
            S.add("act", R.copy(out=Vaug[:, t, :, 0:128], in_=pj[0][:].rearrange("p (h e) -> p h e", h=4)), reads=["pj0"], writes=[f"V{t}"])
        if t == TS:
            S.add("sp", R.dma_start(out=nk_s, in_=kvs[0:4, 0:512]), reads=[f"kvs{sl}k"], dma=True)
            S.add("sp", R.dma_start(out=nv_s, in_=kvs[0:4, 512:1024]), reads=[f"kvs{sl}v"], dma=True)
        else:
            nr = 16 if t == 16 else 128
            S.add("sp", R.dma_start(out=nk_p[t * 128:t * 128 + nr, :], in_=kvs[0:nr, 0:512]), reads=[f"kvs{sl}k"], dma=True)
            S.add("sp", R.dma_start(out=nv_p[t * 128:t * 128 + nr, :], in_=kvs[0:nr, 512:1024]), reads=[f"kvs{sl}v"], dma=True)
        proj(3, 1)
        proj(4, 0)
        ut = u_tm[sl]
        S.add("act", R.activation(out=e_t[:], in_=pj[0][:], func=AF.Exp, scale=-1.0), reads=["pj0"], writes=["z2"])
        S.add("dve", R.tensor_scalar(out=e_t[:], in0=e_t[:], scalar1=1.0, scalar2=None, op0=ALU.add), reads=["z2"], writes=["z2"])
        S.add("dve", R.reciprocal(out=e_t[:], in_=e_t[:]), reads=["z2"], writes=["z2"])
        S.add("dve", R.tensor_tensor(out=ut[:], in0=pj[1][:], in1=e_t[:], op=ALU.mult), reads=["pj1", "z2"], writes=[f"u_tm{sl}"])
        if t == 15:
            S.add("sp", R.dma_start(out=nc_p[0:14, :], in_=ut[114:128, :]), reads=[f"u_tm{sl}"], dma=True)
        if t == 16:
            S.add("sp", R.dma_start(out=nc_p[14:30, :], in_=ut[0:16, :]), reads=[f"u_tm{sl}"], dma=True)
        if t == TS:
            for c in range(8):
                S.add("pe", R.transpose(out=ptr[:, c, :], in_=qk_tm[:, c * 128:(c + 1) * 128], identity=identb[:]), reads=["qk_q", "qk_k", "identb"], writes=["ptr"])
            S.add("act", R.copy(out=qkTT[:], in_=ptr[:]), reads=["ptr"], writes=["qkTT"])
            return
        for i in range(8):
            S.add("pe", R.transpose(out=ptr[0:64, i, :], in_=qk_tm[:, i * 64:(i + 1) * 64], identity=identb[:]), reads=["qk_q", "identb"], writes=["ptr", "ptrq"])
        S.add("act", R.copy(out=qT[0:64, :, :], in_=ptr[0:64, :, :]), reads=["ptr"], writes=["qT"])
        S.add("pool", R.dma_start(out=qT[64:68, :, :], in_=c_qx[:, :, t * 128:(t + 1) * 128]), writes=["qTx"], dma=True)
        for i in range(8):
            S.add("pe", R.transpose(out=ptr[0:64, i, :], in_=qk_tm[:, 512 + i * 64:512 + (i + 1) * 64], identity=identb[:]), reads=["qk_k", "identb"], writes=["ptr", "ptrk"])
        S.add("act", R.copy(out=kT[0:64, :, t * 128:(t + 1) * 128], in_=ptr[0:64, :, :]), reads=["ptr"], writes=[f"kT{t}"])

    def post_attn(h, num0, num1, rs0, rs1, keys, n=128):
        S.add("dve", R.tensor_scalar(out=t1[0:n], in0=num1, scalar1=rs1, scalar2=lam[0:n, 3:4], op0=ALU.mult, op1=ALU.mult), reads=keys + ["lam"], writes=["t1"])
        S.add("dve", R.scalar_tensor_tensor(out=o_h[0:n], in0=num0, scalar=rs0, in1=t1[0:n], op0=ALU.mult, op1=ALU.add), reads=keys + ["t1"], writes=["o_h"])
        S.add("act", R.activation(out=junk[0:n, 0:128], in_=o_h[0:n], func=AF.Square, accum_out=st[0:n, 6:7]), reads=["o_h"], writes=["junk", "st6"])
        S.add("act", R.activation(out=st[0:n, 7:8], in_=st[0:n, 6:7], func=AF.Ln, bias=eps_t[0:n, 0:1], scale=1.0 / 128), reads=["st6", "eps"], writes=["st7"])
        S.add("act", R.activation(out=st[0:n, 7:8], in_=st[0:n, 7:8], func=AF.Exp, scale=-0.5), reads=["st7"], writes=["st7"])
        S.add("dve", R.scalar_tensor_tensor(out=cat_tm[0:n, h * 128:(h + 1) * 128], in0=o_h[0:n], scalar=st[0:n, 7:8], in1=subln_bc[0:n], op0=ALU.mult, op1=ALU.mult), reads=["o_h", "st7", "subln"], writes=[f"cat{h}"])

    def post_attn_prompt(h, pb):
        S.add("dve", R.reciprocal(out=st[:, 4:6], in_=pO[pb][:, :, 128]), reads=[f"pO{pb}"], writes=["st45"])
        post_attn(h, pO[pb][:, 0, 0:128], pO[pb][:, 1, 0:128], st[:, 4:5], st[:, 5:6], [f"pO{pb}", "st45"])

    cnt = {"pst": 0}
    def attn_head(qt, h):
        pb = h % 2
        for j in range(2):
            pr = 2 * h + j
            for g0 in range(0, qt + 1, 4):
                kts = list(range(g0, min(g0 + 4, qt + 1)))
                sb_ = cnt["pst"] % 2; cnt["pst"] += 1
                for i, kt in enumerate(kts):
                    o_ap = pst[sb_][:, i * 128:(i + 1) * 128]
                    if kt == qt:
                        S.add("pe", R.matmul(o_ap, lhsT=identb[:], rhs=maskb[:], start=True, stop=False), reads=["identb", "maskb"], writes=[f"pst{sb_}"])
                    S.add("pe", R.matmul(o_ap, lhsT=kT[0:68, pr, kt * 128:(kt + 1) * 128], rhs=qT[0:68, pr, :], start=(kt != qt), stop=True), reads=[f"kT{kt}", "kTx", "qT", "qTx"], writes=[f"pst{sb_}"])
                n = len(kts) * 128
                S.add("act", R.activation(out=PT[sb_][:, 0:n], in_=pst[sb_][:, 0:n], func=AF.Exp), reads=[f"pst{sb_}"], writes=[f"PT{sb_}"])
                for i, kt in enumerate(kts):
                    S.add("pe", R.matmul(pO[pb][:, j, :], lhsT=PT[sb_][:, i * 128:(i + 1) * 128], rhs=Vaug[:, kt, h, :], start=(kt == 0), stop=(kt == qt)), reads=[f"PT{sb_}", f"V{kt}", "Vones"], writes=[f"pO{pb}"])
        post_attn_prompt(h, pb)

    def conv_u(t):
        sl = 0
        for c in range(4):
            S.add("pe", R.transpose(out=ptf[:, c * 128:(c + 1) * 128], in_=u_tm[sl][:, c * 128:(c + 1) * 128], identity=identf[:]), reads=[f"u_tm{sl}", "identf"], writes=["ptf"])
        S.add("act", R.copy(out=uT[:, :, 30:158], in_=ptf[:].rearrange("p (c t) -> p c t", c=4)), reads=["ptf"], writes=["uT"])

    def conv_taps(j0, j1):
        for j in range(j0, j1):
            for c in range(4):
                if j == 0:
                    S.add("dve", R.tensor_scalar(out=yT[:, c, :], in0=uT[:, c, j:j + 128], scalar1=convwT[:, c, j:j + 1], scalar2=None, op0=ALU.mult), reads=["uT", "convwT"], writes=[f"yT{c}"])
                else:
                    S.add("dve", R.scalar_tensor_tensor(out=yT[:, c, :], in0=uT[:, c, j:j + 128], scalar=convwT[:, c, j:j + 1], in1=yT[:, c, :], op0=ALU.mult, op1=ALU.add), reads=["uT", "convwT", f"yT{c}"], writes=[f"yT{c}"])

    def conv_tail(t):
        yk = [f"yT{c}" for c in range(4)]
        S.add("act", R.copy(out=uT[:, :, 0:30], in_=uT[:, :, 128:158]), reads=["uT"] + yk, writes=["uT"])
        for c in range(4):
            S.add("pe", R.transpose(out=ptf[:, c * 128:(c + 1) * 128], in_=yT[:, c, :], identity=identf[:]), reads=[f"yT{c}", "identf"], writes=["ptf"])

    def conv_out(src_ap, src_key, n=128):
        S.add("dve", R.tensor_tensor(out=z_t[0:n], in0=src_ap, in1=convb_bc[0:n], op=ALU.add), reads=[src_key, "convb"], writes=["z_t"])
        S.add("dve", R.reduce_sum(out=st[0:n, 8:9], in_=z_t[0:n], axis=AX.X), reads=["z_t"], writes=["st8"])
        S.add("act", R.activation(out=junk[0:n, 0:512], in_=z_t[0:n], func=AF.Square, accum_out=st[0:n, 9:10]), reads=["z_t"], writes=["junk", "st9"])
        S.add("dve", R.tensor_scalar(out=st[0:n, 8:9], in0=st[0:n, 8:9], scalar1=1.0 / 512, scalar2=None, op0=ALU.mult), reads=["st8"], writes=["st8"])
        S.add("dve", R.tensor_tensor(out=st[0:n, 10:11], in0=st[0:n, 8:9], in1=st[0:n, 8:9], op=ALU.mult), reads=["st8"], writes=["st10"])
        S.add("dve", R.scalar_tensor_tensor(out=st[0:n, 10:11], in0=st[0:n, 10:11], scalar=-512.0, in1=st[0:n, 9:10], op0=ALU.mult, op1=ALU.add), reads=["st10", "st9"], writes=["st10"])
        S.add("act", R.activation(out=st[0:n, 11:12], in_=st[0:n, 10:11], func=AF.Ln, bias=eps_t[0:n, 0:1], scale=1.0 / 512), reads=["st10", "eps"], writes=["st11"])
        S.add("act", R.activation(out=st[0:n, 11:12], in_=st[0:n, 11:12], func=AF.Exp, scale=-0.5), reads=["st11"], writes=["st11"])
        S.add("dve", R.tensor_scalar(out=z_t[0:n], in0=z_t[0:n], scalar1=st[0:n, 8:9], scalar2=st[0:n, 11:12], op0=ALU.subtract, op1=ALU.mult), reads=["z_t", "st8", "st11"], writes=["z_t"])
        S.add("dve", R.tensor_tensor(out=z_t[0:n], in0=z_t[0:n], in1=clnw_bc[0:n], op=ALU.mult), reads=["z_t", "clnw"], writes=["z_t"])
        S.add("dve", R.tensor_tensor(out=z_t[0:n], in0=z_t[0:n], in1=clnb_bc[0:n], op=ALU.add), reads=["z_t", "clnb"], writes=["z_t"])
        S.add("act", R.activation(out=z2[0:n], in_=z_t[0:n], func=AF.Exp, scale=-1.0), reads=["z_t"], writes=["z2"])
        S.add("dve", R.tensor_scalar(out=z2[0:n], in0=z2[0:n], scalar1=1.0, scalar2=None, op0=ALU.add), reads=["z2"], writes=["z2"])
        S.add("dve", R.reciprocal(out=z2[0:n], in_=z2[0:n]), reads=["z2"], writes=["z2"])
        S.add("dve", R.tensor_tensor(out=cat_tm[0:n, 512:1024], in0=z_t[0:n], in1=z2[0:n], op=ALU.mult), reads=["z_t", "z2"], writes=["cat4"])

    def dense_mid(t):
        ck_ = [f"cat{h}" for h in range(5)]
        for k in range(8):
            S.add("pe", R.transpose(out=ptr[:, k, :], in_=cat_tm[:, k * 128:(k + 1) * 128], identity=identb[:]), reads=ck_ + ["identb"], writes=["ptr"])
        S.add("act", R.copy(out=catT[:], in_=ptr[:]), reads=["ptr"], writes=["catT"])
        wk = [f"w_out{k}" for k in range(8)]
        for hf in range(2):
            for k in range(8):
                S.add("pe", R.matmul(pj[hf][:], lhsT=catT[:, k, :], rhs=w_out_b[:, k, hf * 512:(hf + 1) * 512], start=(k == 0), stop=(k == 7)), reads=["catT"] + wk, writes=[f"pj{hf}"])
            S.add("act", R.activation(out=junk[:, 0:512], in_=pj[hf][:], func=AF.Square, accum_out=st[:, 12 + hf:13 + hf]), reads=[f"pj{hf}"], writes=["junk", f"st{12 + hf}"])
        S.add("dve", R.tensor_tensor(out=st[:, 14:15], in0=st[:, 12:13], in1=st[:, 13:14], op=ALU.add), reads=["st12", "st13"], writes=["st14"])
        rstd_from_ss(S, st[:, 14:15], 1024, st[:, 15:16], "st14", "st15")
        for hf in range(2):
            S.add("dve", R.scalar_tensor_tensor(out=d_all[:, t, hf * 512:(hf + 1) * 512], in0=pj[hf][:], scalar=st[:, 15:16], in1=gpost_bc[:, hf * 512:(hf + 1) * 512], op0=ALU.mult, op1=ALU.mult), reads=[f"pj{hf}", "st15", "gpost"], writes=[f"d{t}"])

    parts = [(0, 8), (8, 16), (16, 24), (24, 31)]
    if DBG_TILES > 0:
        dense_in(0)
    for t in range(DBG_TILES):
        if t + 1 < DBG_TILES:
            pre_x(t + 1)
        conv_u(t)
        for h in range(4):
            conv_taps(*parts[h])
            attn_head(t, h)
        if t + 1 < DBG_TILES:
            dense_proj(t + 1)
        conv_tail(t)
        conv_out(ptf[:], "ptf")
        dense_mid(t)

    S.emit()
    st1a.close()
    if STOP_AFTER == '1a':
        st1.close(); outer.close(); _SEM_STACKS.clear(); return nc
    S = Sched(nc)
    st1b = contextlib.ExitStack()
    ptf = ps("ptf_b", [128, 512], F32, st1b)
    kTp = [ps(f"kTp{i}", [128, 8, 128], BF16, st1b) for i in range(2)]
    pss = [ps(f"pss{i}", [128, 512], F32, st1b) for i in range(2)]
    ptf32 = sb("ptf32", [128, 4, 64], F32, st1b); pti = sb("pti", [128, 4, 64], I32, st1b)
    idxf = sb("idxf", [128, 4, 64], F32, st1b); idxi = sb("idxi", [128, 4, 64], I32, st1b)
    NB = 3
    kpg = [sb(f"kpg{i}", [128, 512], BF16, st1b) for i in range(NB)]; vpg = [sb(f"vpg{i}", [128, 512], BF16, st1b) for i in range(NB)]
    kTs = [sb(f"kTs{i}", [128, 4, 128], BF16, st1b) for i in range(2)]
    Vn = sb("Vn", [128, 512], BF16, st1b); qkTT = sb("qkTT", [128, 8, 128], BF16, st1b)
    Qbd = [sb(f"Qbd{i}", [128, 4, 2], BF16, st1b) for i in range(4)]
    Ps = [[sb(f"Ps{i}_{b}", [128, 8, 4], BF16, st1b) for b in range(2)] for i in range(4)]
    posT = sb("posT", [2, 128], BF16, st1b); abias = sb("abias", [2, 65, 8], BF16, st1b)
    mnew = sb("mnew", [128, 4, 8], BF16, st1b); ones4 = sb("ones4", [128, 4], BF16, st1b)
    dmask = sb("dmask", [128, 8, 4], F32, st1b); ssum = sb("ssum", [128, 8, 4], F32, st1b); rsum = sb("rsum", [128, 8], F32, st1b)
    ufull = [sb(f"ufull{i}", [31, 512], F32, st1b) for i in range(2)]; prod = sb("prod", [31, 512], F32, st1b)
    S.add("pool", R.dma_start(out=posT[:], in_=c_posT), writes=["posT"], dma=True)
    S.add("pool", R.dma_start(out=abias[:], in_=c_abias), writes=["abias"], dma=True)
    S.add("pool", R.dma_start(out=mnew[:], in_=c_mnew), writes=["mnew"], dma=True)
    S.add("sp", R.dma_start(out=dmask[:], in_=c_dmask), writes=["dmask"], dma=True)
    S.add("pool", R.memset(ones4[:], 1.0), writes=["ones4"])
    S.add("pool", R.memset(cat_tm[:], 0.0), writes=["cat0", "cat1", "cat2", "cat3", "cat4"])
    for i in range(4):
        S.add("pool", R.memset(Qbd[i][:], 0.0), writes=[f"Qbd{i}"])
        for b in range(2):
            S.add("pool", R.memset(Ps[i][b][:], 0.0), writes=[f"Ps{i}_{b}"])
    dense_in(TS)
    sl = 0
    S.add("sp", R.dma_start(out=pti[:].rearrange("p s j -> p (s j)"), in_=ptab.partition_broadcast(128)), writes=["pti"], dma=True)
    S.add("dve", R.tensor_copy(out=ptf32[:], in_=pti[:]), reads=["pti"], writes=["ptf32"])
    S.add("dve", R.tensor_scalar(out=idxf[:], in0=ptf32[:], scalar1=128.0, scalar2=iot[:, 0:1], op0=ALU.mult, op1=ALU.add), reads=["ptf32", "iot"], writes=["idxf"])
    S.add("dve", R.tensor_copy(out=idxi[:], in_=idxf[:]), reads=["idxf"], writes=["idxi"])
    for s in range(4):
        S.add("dve", R.tensor_copy(out=Qbd[s][0:64, :, 0], in_=qkTT[0:64, 0:4, s]), reads=["qkTT", f"Qbd{s}"], writes=[f"Qbd{s}"])
        S.add("dve", R.tensor_copy(out=Qbd[s][64:128, :, 1], in_=qkTT[64:128, 0:4, s]), reads=["qkTT", f"Qbd{s}"], writes=[f"Qbd{s}"])
    zeros8 = sb("zeros8", [128, 8], BF16, st1b); shiftm = sb("shiftm", [8, 4], F32, st1b); accS = sb("accS", [8, 512], F32, st1b)
    S.add("pool", R.memset(zeros8[:], 0.0), writes=["zeros8"])
    S.add("sp", R.dma_start(out=shiftm[:], in_=c_shift), writes=["shiftm"], dma=True)
    S.add("pe", R.matmul(pj[0][0:8, :], lhsT=zeros8[:], rhs=w_out_b[:, 0, 0:512], start=True, stop=False), reads=["zeros8", "qkTT", "Vn"], writes=["pj0"])
    it = 0
    nsteps = 4 * (NPAGE + 1)
    for s in range(4):
        for jp in range(NPAGE + 1):
            b3 = it % NB; b2 = it % 2
            if jp < NPAGE:
                S.add("pool", R.indirect_dma_start(out=kpg[b3][:], out_offset=None, in_=ck, in_offset=bass.IndirectOffsetOnAxis(ap=idxi[:, s, jp:jp + 1], axis=0)), reads=["idxi"], writes=[f"kpg{b3}"], dma=True)
                S.add("pool", R.indirect_dma_start(out=vpg[b3][:], out_offset=None, in_=cv, in_offset=bass.IndirectOffsetOnAxis(ap=idxi[:, s, jp:jp + 1], axis=0)), reads=["idxi"], writes=[f"vpg{b3}"], dma=True)
                for c in range(4):
                    S.add("pe", R.transpose(out=kTp[b2][:, c, :], in_=kpg[b3][:, c * 128:(c + 1) * 128], identity=identb[:]), reads=[f"kpg{b3}"], writes=[f"kTp{b2}"])
                S.add("act", R.copy(out=kTs[b2][:], in_=kTp[b2][:, 0:4, :]), reads=[f"kTp{b2}"], writes=[f"kTs{b2}"])
                kt_ = lambda c: kTs[b2][:, c, :]
                kkeys = [f"kTs{b2}"]
                vt_ = vpg[b3]; vkeys = [f"vpg{b3}"]
            else:
                kt_ = lambda c: qkTT[:, 4 + c, :]
                kkeys = ["qkTT"]
                vt_ = Vn; vkeys = ["Vn"]
            if jp < NPAGE:
                S.add("pe", R.matmul(pss[b2][:, 0:8], lhsT=posT[:, :], rhs=abias[:, jp, :], start=True, stop=False), reads=["posT", "abias"], writes=[f"pss{b2}"])
            else:
                S.add("pe", R.matmul(pss[b2][:, 0:8], lhsT=identb[:], rhs=mnew[:, s, :], start=True, stop=False), reads=["mnew"], writes=[f"pss{b2}"])
            for c in range(4):
                S.add("pe", R.matmul(pss[b2][:, 2 * c:2 * c + 2], lhsT=kt_(c), rhs=Qbd[s][:, c, :], start=False, stop=(c == 3)), reads=kkeys + [f"Qbd{s}"], writes=[f"pss{b2}"])
            S.add("act", R.activation(out=Ps[s][b2][:, :, s], in_=pss[b2][:, 0:8], func=AF.Exp), reads=[f"pss{b2}"], writes=[f"Ps{s}_{b2}"])
            first = (it == 0); last = (it == nsteps - 1)
            for h in range(4):
                S.add("pe", R.matmul(pj[0][0:8, h * 128:(h + 1) * 128], lhsT=Ps[s][b2][:, 2 * h:2 * h + 2, :].rearrange("p a b -> p (a b)"), rhs=vt_[:, h * 128:(h + 1) * 128], start=False, stop=(last and h == 3)), reads=[f"Ps{s}_{b2}"] + vkeys, writes=["pj0"])
            S.add("pe", R.matmul(ptf[0:4, 0:32], lhsT=ones4[:], rhs=Ps[s][b2][:].rearrange("p a b -> p (a b)"), start=first, stop=last), reads=[f"Ps{s}_{b2}", "ones4"], writes=["ptf"])
            it += 1
    S.add("dve", R.tensor_tensor(out=ssum[0:4], in0=ptf[0:4, 0:32].rearrange("p (a b) -> p a b", a=8), in1=dmask[0:4], op=ALU.mult), reads=["ptf", "dmask"], writes=["ssum"])
    S.add("dve", R.reduce_sum(out=rsum[0:4], in_=ssum[0:4], axis=AX.X), reads=["ssum"], writes=["rsum"])
    S.add("dve", R.reciprocal(out=rsum[0:4], in_=rsum[0:4]), reads=["rsum"], writes=["rsum"])
    S.add("act", R.copy(out=accS[:], in_=pj[0][0:8, :]), reads=["pj0"], writes=["accS"])
    S.add("pe", R.matmul(pj[1][0:4, :], lhsT=shiftm[:], rhs=accS[:], start=True, stop=True), reads=["accS", "shiftm"], writes=["pj1"])
    for h in range(4):
        post_attn(h, pj[0][0:4, h * 128:(h + 1) * 128], pj[1][0:4, h * 128:(h + 1) * 128], rsum[0:4, 2 * h:2 * h + 1], rsum[0:4, 2 * h + 1:2 * h + 2], ["pj0", "pj1", "rsum"], n=4)
    for s in range(4):
        ub = ufull[s % 2]
        S.add("sp", R.dma_start(out=ub[0:30, :], in_=state[s]), writes=[f"ufull{s % 2}"], dma=True)
        S.add("sp", R.dma_start(out=ub[30:31, :], in_=u_tm[sl][s:s + 1, :]), reads=[f"u_tm{sl}"], writes=[f"ufull{s % 2}b"], dma=True)
        S.add("sp", R.dma_start(out=nc_s[s], in_=ub[1:31, :]), reads=[f"ufull{s % 2}", f"ufull{s % 2}b"], dma=True)
        S.add("dve", R.tensor_tensor(out=prod[:], in0=ub[:], in1=convw_tm[:], op=ALU.mult), reads=[f"ufull{s % 2}", f"ufull{s % 2}b", "convw_tm"], writes=["prod"])
        S.add("pe", R.matmul(ptf[0:4, :], lhsT=sel[:, s, :], rhs=prod[:], start=(s == 0), stop=(s == 3)), reads=["prod", "sel"], writes=["ptf"])
    conv_out(ptf[0:4, :], "ptf", n=4)
    dense_mid(TS)
    S.emit()
    st1b.close()
    st1.close()
    if STOP_AFTER == '1b':
        outer.close(); _SEM_STACKS.clear(); return nc

    st2 = contextlib.ExitStack()
    S = Sched(nc)
    wg = sb("wg", [128, 8, 2816], BF16, st2); wu = sb("wu", [128, 8, 2816], BF16, st2); wd = sb("wd", [128, 22, 1024], BF16, st2)
    gfpre_bc = sb("gfpre_bc", [128, 1024], F32, st2); gfpost_bc = sb("gfpost_bc", [128, 1024], F32, st2)
    xs2 = [sb(f"xm{i}", [128, 1024], F32, st2) for i in range(2)]
    junk2 = sb("junk2", [128, 1024], BF16, st2); xn2 = sb("xn2", [128, 1024], BF16, st2); hTs = [sb(f"hT{i}", [128, 8, 128], BF16, st2) for i in range(2)]
    actT = sb("actT", [128, 22, 128], BF16, st2); gsl = [sb(f"g2_{i}", [128, 512], F32, st2) for i in range(2)]
    st_ = sb("stat2", [128, 8], F32, st2); ob = [sb("ob0", [128, 1024], F32, st2)] * 2
    pf = [ps(f"pf{i}", [128, 512], F32, st2) for i in range(2)]
    pk = [ps(f"pk{i}", [128, 512], F32, st2) for i in range(2)]
    for k in range(8):
        S.add("pool", R.dma_start(out=wg[:, k, :], in_=w_gate[k * 128:(k + 1) * 128, :]), writes=[f"wg{k}"], dma=True)
        S.add("pool", R.dma_start(out=wu[:, k, :], in_=w_up[k * 128:(k + 1) * 128, :]), writes=[f"wu{k}"], dma=True)
    for f in range(22):
        S.add("pool", R.dma_start(out=wd[:, f, :], in_=w_down[f * 128:(f + 1) * 128, :]), writes=[f"wd{f}"], dma=True)
    S.add("sp", R.dma_start(out=gfpre_bc[:], in_=g_fpre.partition_broadcast(128)), writes=["gfpre"], dma=True)
    S.add("sp", R.dma_start(out=gfpost_bc[:], in_=g_fpost.partition_broadcast(128)), writes=["gfpost"], dma=True)
    wgk = [f"wg{k}" for k in range(8)]; wuk = [f"wu{k}" for k in range(8)]; wdk = [f"wd{f}" for f in range(22)]
    tiles = list(DBG_FFN_TILES if DBG_FFN_TILES is not None else range(NTT))

    def pre(t):
        sl = t % 2
        xm = xs2[sl]; h_t = hTs[sl]
        S.add("sp", R.dma_start(out=xm[:], in_=xsrc(t)), writes=[f"xm{sl}"], dma=True)
        S.add("dve", R.tensor_tensor(out=xm[:], in0=xm[:], in1=d_all[:, t, :], op=ALU.add), reads=[f"xm{sl}"], writes=[f"xm{sl}"])
        S.add("act", R.activation(out=junk2[:], in_=xm[:], func=AF.Square, accum_out=st_[:, 0:1]), reads=[f"xm{sl}"], writes=["junk2", "s0"])
        rstd_from_ss(S, st_[:, 0:1], 1024, st_[:, 1:2], "s0", "s1")
        S.add("dve", R.scalar_tensor_tensor(out=xn2[:], in0=xm[:], scalar=st_[:, 1:2], in1=gfpre_bc[:], op0=ALU.mult, op1=ALU.mult), reads=[f"xm{sl}", "s1", "gfpre"], writes=["xn2"])
        for k in range(8):
            S.add("pe", R.transpose(out=ptr[:, k, :], in_=xn2[:, k * 128:(k + 1) * 128], identity=identb[:]), reads=["xn2"], writes=["ptr"])
        S.add("act", R.copy(out=h_t[:], in_=ptr[:]), reads=["ptr"], writes=[f"hT{sl}"])

    def gateup(t):
        sl = t % 2
        h_t = hTs[sl]
        ncol = 16 if t == 16 else (4 if t == TS else 128)
        for gi, f0 in enumerate(range(0, 22, 4)):
            fs = list(range(f0, min(f0 + 4, 22)))
            n = len(fs) * 128
            pg, pu = (pj[0], pj[1]) if gi % 2 == 0 else (pk[0], pk[1])
            kg, ku = ("pj0", "pj1") if gi % 2 == 0 else ("pk0", "pk1")
            gs = gsl[gi % 2]
            for i, f in enumerate(fs):
                for k in range(8):
                    S.add("pe", R.matmul(pg[:, i * 128:i * 128 + ncol], lhsT=wg[:, k, f * 128:(f + 1) * 128], rhs=h_t[:, k, 0:ncol], start=(k == 0), stop=(k == 7)), reads=[f"hT{sl}"] + wgk, writes=[kg])
            for i, f in enumerate(fs):
                for k in range(8):
                    S.add("pe", R.matmul(pu[:, i * 128:i * 128 + ncol], lhsT=wu[:, k, f * 128:(f + 1) * 128], rhs=h_t[:, k, 0:ncol], start=(k == 0), stop=(k == 7)), reads=[f"hT{sl}"] + wuk, writes=[ku])
            S.add("act", R.activation(out=gs[:, 0:n], in_=pg[:, 0:n], func=AF.Silu), reads=[kg], writes=[f"g2_{gi % 2}"])
            S.add("dve", R.tensor_tensor(out=actT[:, f0:f0 + n // 128, :].rearrange("p f t -> p (f t)"), in0=pu[:, 0:n], in1=gs[:, 0:n], op=ALU.mult), reads=[ku, f"g2_{gi % 2}"], writes=[f"actT{f0}"])

    def down_post(t):
        sl = t % 2
        xm = xs2[sl]
        ak = [f"actT{f0}" for f0 in range(0, 22, 4)]
        for hf in range(2):
            for f in range(22):
                S.add("pe", R.matmul(pf[hf][:], lhsT=actT[:, f, :], rhs=wd[:, f, hf * 512:(hf + 1) * 512], start=(f == 0), stop=(f == 21)), reads=ak + wdk, writes=[f"pf{hf}"])
            S.add("act", R.activation(out=junk2[:, 0:512], in_=pf[hf][:], func=AF.Square, accum_out=st_[:, 2 + hf:3 + hf]), reads=[f"pf{hf}"], writes=["junk2", f"s{2 + hf}"])
        S.add("dve", R.tensor_tensor(out=st_[:, 4:5], in0=st_[:, 2:3], in1=st_[:, 3:4], op=ALU.add), reads=["s2", "s3"], writes=["s4"])
        rstd_from_ss(S, st_[:, 4:5], 1024, st_[:, 5:6], "s4", "s5")
        o_t = ob[0]
        for hf in range(2):
            S.add("dve", R.scalar_tensor_tensor(out=o_t[:, hf * 512:(hf + 1) * 512], in0=pf[hf][:], scalar=st_[:, 5:6], in1=gfpost_bc[:, hf * 512:(hf + 1) * 512], op0=ALU.mult, op1=ALU.mult), reads=[f"pf{hf}", "s5", "gfpost"], writes=["ob0"])
        S.add("dve", R.tensor_tensor(out=o_t[:], in0=o_t[:], in1=xm[:], op=ALU.add), reads=["ob0", f"xm{sl}"], writes=["ob0"])
        if t == 0:
            S.add("sp", R.dma_start(out=y_p[0:112, :], in_=o_t[16:128, :]), reads=["ob0"], dma=True)
        elif t < 16:
            S.add("sp", R.dma_start(out=y_p[t * 128 - 16:t * 128 + 112, :], in_=o_t[:, :]), reads=["ob0"], dma=True)
        elif t == 16:
            S.add("sp", R.dma_start(out=y_p[2032:2048, :], in_=o_t[0:16, :]), reads=["ob0"], dma=True)
        else:
            S.add("sp", R.dma_start(out=y_s, in_=o_t[0:4, :]), reads=["ob0"], dma=True)

    pre(tiles[0])
    for i, t in enumerate(tiles):
        gateup(t)
        if i + 1 < len(tiles):
            pre(tiles[i + 1])
        down_post(t)
    S.emit()
    st2.close()
    for s_ in _SEM_STACKS:
        s_.close()
    _SEM_STACKS.clear()
    outer.close()
    return nc


def _consts():
    c = {}
    c["c_identf"] = np.eye(128, dtype=np.float32)
    k = np.arange(128)[:, None]; q = np.arange(128)[None, :]
    c["c_mask"] = np.where(k > q, -30000.0, 0.0).astype(np.float32)
    npos = NT * 128
    pos = np.arange(npos); a = pos // 128; b = pos % 128
    qx = np.zeros((4, 8, npos), np.float32); kx = np.zeros((4, 8, npos), np.float32)
    for pr in range(8):
        s = SL[pr // 2]
        qx[0, pr] = -s * 128 * a; qx[1, pr] = -s * b; qx[2, pr] = 1; qx[3, pr] = 1
        kx[0, pr] = 1; kx[1, pr] = 1; kx[2, pr] = s * 128 * a; kx[3, pr] = s * b
    c["c_qx"] = qx; c["c_kx"] = kx
    posT = np.zeros((2, 128), np.float32); posT[0] = 1.0; posT[1] = np.arange(128)
    abias = np.zeros((2, 65, 8), np.float32)
    for pr in range(8):
        s = SL[pr // 2]
        abias[0, :64, pr] = -s * (8192 - 128 * np.arange(64)); abias[1, :64, pr] = s
    mnew = np.full((128, 4, 8), -30000.0, np.float32)
    dmask = np.zeros((128, 8, 4), np.float32)
    for s in range(4):
        mnew[s, s, :] = 0.0
        dmask[s, :, s] = 1.0
    shiftm = np.zeros((8, 4), np.float32)
    for s in range(4):
        shiftm[4 + s, s] = 1.0
    c["c_shift"] = shiftm
    c["c_posT"] = posT; c["c_abias"] = abias; c["c_mnew"] = mnew; c["c_dmask"] = dmask
    sel = np.zeros((31, 4, 4), np.float32)
    for s in range(4):
        sel[:, s, s] = 1.0
    c["c_sel"] = sel
    c["c_iota"] = np.arange(128, dtype=np.float32).reshape(128, 1)
    return c


def kernel(x_prompt, x_sample, cache_k, cache_v, state_conv, page_table, meta_tokens,
           ln_mix_pre, ln_mix_post, w_in, lambda_q1, lambda_k1, lambda_q2, lambda_k2,
           subln_w, conv_w, conv_b, conv_ln_w, conv_ln_b, w_out, ln_ffn_pre, ln_ffn_post,
           w_gate, w_up, w_down):
    f = lambda a: np.ascontiguousarray(np.asarray(a, dtype=np.float32))
    nc = build()
    consts = _consts()
    ck = f(cache_k).reshape(2560 * 128, 512)[:CK_ROWS]; cv = f(cache_v).reshape(2560 * 128, 512)[:CK_ROWS]
    shared = {
        "ck": ck, "cv": cv, "w_in": f(w_in)[0], "w_out": f(w_out)[0], "w_gate": f(w_gate)[0], "w_up": f(w_up)[0], "w_down": f(w_down)[0],
        "g_pre": f(ln_mix_pre), "g_post": f(ln_mix_post), "g_fpre": f(ln_ffn_pre), "g_fpost": f(ln_ffn_post),
        "subln": f(subln_w), "conv_w": f(conv_w)[0], "conv_b": f(conv_b), "cln_w": f(conv_ln_w), "cln_b": f(conv_ln_b),
        "lq1": f(lambda_q1), "lk1": f(lambda_k1), "lq2": f(lambda_q2), "lk2": f(lambda_k2),
    }
    shared.update(consts)
    xpn = f(x_prompt); xsn = f(x_sample); meta = f(meta_tokens); stc = f(state_conv)[0]
    pt = np.ascontiguousarray(np.asarray(page_table, dtype=np.int32))
    in_maps = []
    for c in range(NCORES):
        xp = np.zeros((NT * 128, 1024), np.float32)
        xp[0:16] = meta; xp[16:2064] = xpn[c]
        xs4 = np.zeros((128, 1024), np.float32); xs4[0:4] = xsn[4 * c:4 * c + 4, 0]
        m = dict(shared)
        m.update({"xp": xp, "xs4": xs4, "ptab": np.ascontiguousarray(pt[4 * c:4 * c + 4]).reshape(1, 256), "state": np.ascontiguousarray(stc[4 * c:4 * c + 4])})
        in_maps.append(m)
    res = run_bass_kernel_spmd(nc, in_maps, core_ids=list(range(NCORES)))
    R = res.results
    y_p = np.stack([R[c]["y_p"] for c in range(NCORES)])
    y_s = np.concatenate([R[c]["y_s"] for c in range(NCORES)])[:, None, :]
    nk_p = np.stack([R[c]["nk_p"] for c in range(NCORES)]).reshape(1, NCORES, 2064, 8, 64)
    nv_p = np.stack([R[c]["nv_p"] for c in range(NCORES)]).reshape(1, NCORES, 2064, 4, 128)
    nc_p = np.stack([R[c]["nc_p"] for c in range(NCORES)]).reshape(1, NCORES, 30, 512)
    nk_s = np.concatenate([R[c]["nk_s"] for c in range(NCORES)]).reshape(1, 4 * NCORES, 1, 8, 64)
    nv_s = np.concatenate([R[c]["nv_s"] for c in range(NCORES)]).reshape(1, 4 * NCORES, 1, 4, 128)
    nc_s = np.concatenate([R[c]["nc_s"] for c in range(NCORES)]).reshape(1, 4 * NCORES, 30, 512)
    return (y_p.astype(np.float32), y_s.astype(np.float32), nk_p, nv_p, nc_p, nk_s, nv_s, nc_s)
```
